# Optimizing a Trainium2 kernel written in Bass

```python
import math
import jax, jax.numpy as jnp
from jax import lax
import numpy as np

D_MODEL = 1024
BATCH = 8
SEQ = 4096
DEPTH = 2

CHUNK = 64
N_HEADS = 8
N_KV_HEADS = 2
HEAD_DIM = 64
Q_PER_KV = N_HEADS // N_KV_HEADS
WINDOW = 128
WIN_CHUNKS = WINDOW // CHUNK
ATT_W = N_HEADS * HEAD_DIM
KV_W = N_KV_HEADS * HEAD_DIM
SSM_W = 512
SSM_GROUP = 16
SSM_GROUPS = SSM_W // SSM_GROUP
SSM_STATE = 64
POOL_W = 512
POOL_WINDOWS = (2, 4, 8, 16)
POOL_GROUPS = len(POOL_WINDOWS)
POOL_GW = POOL_W // POOL_GROUPS
N_BRANCH = 3
SPLIT_SIZES = (ATT_W, KV_W, KV_W, SSM_W, POOL_W, ATT_W, SSM_W, POOL_W, N_BRANCH * D_MODEL)
IN_W = sum(SPLIT_SIZES)
EPS = 1e-6
NEG_INF = -1e30

kernel_name = "hybrid_gated_swa_s5_pool_adaln"


def rmsnorm(x, g):
    xf = x.astype(jnp.float32)
    y = xf * lax.rsqrt(jnp.mean(xf * xf, axis=-1, keepdims=True) + EPS)
    return (y * g.astype(jnp.float32)).astype(x.dtype)


def alibi_slopes(n):
    return jnp.asarray([2.0 ** (-8.0 * (h + 1) / n) for h in range(n)], dtype=jnp.float32)


def window_attention(q, k, v, sinks):
    b, l = q.shape[:2]
    nc = l // CHUNK
    pad = WIN_CHUNKS * CHUNK
    nk = (WIN_CHUNKS + 1) * CHUNK
    kp = jnp.pad(k, ((0, 0), (pad, 0), (0, 0), (0, 0))).reshape(b, nc + WIN_CHUNKS, CHUNK, N_KV_HEADS, HEAD_DIM)
    vp = jnp.pad(v, ((0, 0), (pad, 0), (0, 0), (0, 0))).reshape(b, nc + WIN_CHUNKS, CHUNK, N_KV_HEADS, HEAD_DIM)
    kb = jnp.concatenate([kp[:, j:j + nc] for j in range(WIN_CHUNKS + 1)], axis=2)
    vb = jnp.concatenate([vp[:, j:j + nc] for j in range(WIN_CHUNKS + 1)], axis=2)
    qb = q.reshape(b, nc, CHUNK, N_KV_HEADS, Q_PER_KV, HEAD_DIM)
    s = jnp.einsum('bcqkgd,bcskd->bckgqs', qb, kb).astype(jnp.float32) * (1.0 / math.sqrt(HEAD_DIM))
    qi = jnp.arange(CHUNK)[:, None]
    kj = jnp.arange(nk)[None, :]
    dist = jnp.abs(qi + pad - kj).astype(jnp.float32)
    slopes = alibi_slopes(N_HEADS).reshape(N_KV_HEADS, Q_PER_KV)
    s = s - slopes[:, :, None, None] * dist[None, None]
    valid = (jnp.arange(nc)[:, None] * CHUNK + jnp.arange(nk)[None, :]) >= pad
    s = jnp.where(valid[None, :, None, None, None, :], s, NEG_INF)
    sink = jnp.broadcast_to(sinks.astype(jnp.float32).reshape(N_KV_HEADS, Q_PER_KV)[None, None, :, :, None, None],
                            s.shape[:-1] + (1,))
    p = jax.nn.softmax(jnp.concatenate([s, sink], axis=-1), axis=-1)[..., :-1]
    o = jnp.einsum('bckgqs,bcskd->bcqkgd', p.astype(v.dtype), vb)
    return o.reshape(b, l, ATT_W)


def s5_layer(u, a_re, a_im, log_dt, b_re, b_im, c_re, c_im, d_skip, w_glu, b_glu):
    b, l = u.shape[:2]
    uf = u.astype(jnp.float32)
    lam = lax.complex(a_re.astype(jnp.float32), a_im.astype(jnp.float32))
    dt = jnp.exp(log_dt.astype(jnp.float32))[:, None]
    lam_bar = jnp.exp(lam * dt)
    bmat = lax.complex(b_re.astype(jnp.float32), b_im.astype(jnp.float32))
    b_bar = ((lam_bar - 1.0) / lam)[..., None] * bmat
    ug = uf.reshape(b, l, SSM_GROUPS, SSM_GROUP).astype(jnp.complex64)
    bu = jnp.einsum('gpc,blgc->blgp', b_bar, ug)
    a = jnp.broadcast_to(lam_bar, bu.shape)

    def combine(e1, e2):
        a1, x1 = e1
        a2, x2 = e2
        return a1 * a2, a2 * x1 + x2

    _, states = lax.associative_scan(combine, (a, bu), axis=1)
    cmat = lax.complex(c_re.astype(jnp.float32), c_im.astype(jnp.float32))
    y = jnp.real(jnp.einsum('gcp,blgp->blgc', cmat, states)).reshape(b, l, SSM_W)
    y = y + d_skip.astype(jnp.float32) * uf
    y = jax.nn.gelu(y)
    y = y * jax.nn.sigmoid(y @ w_glu.astype(jnp.float32) + b_glu.astype(jnp.float32))
    return y.astype(u.dtype)


def multiscale_pool(u, w_pool, pool_scale):
    b, l = u.shape[:2]
    uf = u.astype(jnp.float32).reshape(b, l, POOL_GROUPS, POOL_GW)
    cs = jnp.concatenate([jnp.zeros((b, 1, POOL_GROUPS, POOL_GW), jnp.float32), jnp.cumsum(uf, axis=1)], axis=1)
    t = jnp.arange(l)
    pooled = []
    for gi, w in enumerate(POOL_WINDOWS):
        csp = jnp.pad(cs[:, :, gi], ((0, 0), (w - 1, 0), (0, 0)))
        ssum = csp[:, w:w + l] - csp[:, :l]
        cnt = jnp.minimum(t + 1, w).astype(jnp.float32)[None, :, None]
        pooled.append(ssum / cnt - uf[:, :, gi])
    pooled = jnp.stack(pooled, axis=2)
    y = jnp.einsum('blgi,gio->blgo', pooled, w_pool.astype(jnp.float32)).reshape(b, l, POOL_W)
    return (y * pool_scale.astype(jnp.float32)).astype(u.dtype)


def setup_inputs(seed: int = 0) -> dict:
    key = jax.random.key(seed)
    ks = jax.random.split(key, 32)
    f32 = jnp.float32
    nrm = lambda k, shape, s: jax.random.normal(k, shape, f32) * s
    D = D_MODEL
    n_idx = jnp.arange(SSM_STATE, dtype=f32)
    a_re = -0.5 * (1.0 + 0.02 * jax.random.normal(ks[6], (DEPTH, SSM_GROUPS, SSM_STATE), f32))
    a_im = math.pi * n_idx[None, None, :] + 0.02 * jax.random.normal(ks[7], (DEPTH, SSM_GROUPS, SSM_STATE), f32)
    log_dt = jax.random.uniform(ks[8], (DEPTH, SSM_GROUPS), f32, math.log(1e-3), math.log(1e-1))
    return {
        "x": nrm(ks[0], (BATCH, SEQ, D), 1.0),
        "c": nrm(ks[1], (BATCH, D), 1.0),
        "norm_g": 1.0 + nrm(ks[2], (DEPTH, D), 0.02),
        "w_ada": nrm(ks[3], (DEPTH, D, 3 * D), 0.5 * D ** -0.5),
        "b_ada": nrm(ks[4], (DEPTH, 3 * D), 0.02),
        "w_in": nrm(ks[5], (DEPTH, D, IN_W), D ** -0.5),
        "attn_sinks": nrm(ks[9], (DEPTH, N_HEADS), 0.5),
        "ssm_a_re": a_re,
        "ssm_a_im": a_im,
        "ssm_log_dt": log_dt,
        "ssm_b_re": nrm(ks[10], (DEPTH, SSM_GROUPS, SSM_STATE, SSM_GROUP), (2 * SSM_GROUP) ** -0.5),
        "ssm_b_im": nrm(ks[11], (DEPTH, SSM_GROUPS, SSM_STATE, SSM_GROUP), (2 * SSM_GROUP) ** -0.5),
        "ssm_c_re": nrm(ks[12], (DEPTH, SSM_GROUPS, SSM_GROUP, SSM_STATE), (2 * SSM_STATE) ** -0.5),
        "ssm_c_im": nrm(ks[13], (DEPTH, SSM_GROUPS, SSM_GROUP, SSM_STATE), (2 * SSM_STATE) ** -0.5),
        "ssm_d": nrm(ks[14], (DEPTH, SSM_W), 1.0),
        "w_glu": nrm(ks[15], (DEPTH, SSM_W, SSM_W), SSM_W ** -0.5),
        "b_glu": nrm(ks[16], (DEPTH, SSM_W), 0.02),
        "w_pool": nrm(ks[17], (DEPTH, POOL_GROUPS, POOL_GW, POOL_GW), POOL_GW ** -0.5),
        "pool_scale": 1.0 + nrm(ks[18], (DEPTH, POOL_W), 0.1),
        "w_br_att": nrm(ks[19], (DEPTH, ATT_W, D), ATT_W ** -0.5),
        "w_br_ssm": nrm(ks[20], (DEPTH, SSM_W, D), SSM_W ** -0.5),
        "w_br_pool": nrm(ks[21], (DEPTH, POOL_W, D), POOL_W ** -0.5),
        "w_out": nrm(ks[22], (DEPTH, D, D), D ** -0.5),
        "final_g": 1.0 + nrm(ks[23], (D,), 0.02),
    }


def reference(x, c, norm_g, w_ada, b_ada, w_in, attn_sinks, ssm_a_re, ssm_a_im, ssm_log_dt,
              ssm_b_re, ssm_b_im, ssm_c_re, ssm_c_im, ssm_d, w_glu, b_glu, w_pool, pool_scale,
              w_br_att, w_br_ssm, w_br_pool, w_out, final_g):
    b, l, _ = x.shape
    split_idx = [int(v) for v in np.cumsum(SPLIT_SIZES)[:-1]]
    c_act = jax.nn.silu(c)
    for li in range(DEPTH):
        mod = c_act @ w_ada[li] + b_ada[li]
        shift, scale, gate = jnp.split(mod, 3, axis=-1)
        h = rmsnorm(x, norm_g[li]) * (1.0 + scale[:, None, :]) + shift[:, None, :]
        proj = h @ w_in[li]
        q, k, v, u_ssm, u_pool, z_att, z_ssm, z_pool, g_logits = jnp.split(proj, split_idx, axis=-1)
        y_att = window_attention(q.reshape(b, l, N_HEADS, HEAD_DIM),
                                 k.reshape(b, l, N_KV_HEADS, HEAD_DIM),
                                 v.reshape(b, l, N_KV_HEADS, HEAD_DIM), attn_sinks[li]) * jax.nn.silu(z_att)
        y_ssm = s5_layer(u_ssm, ssm_a_re[li], ssm_a_im[li], ssm_log_dt[li], ssm_b_re[li], ssm_b_im[li],
                         ssm_c_re[li], ssm_c_im[li], ssm_d[li], w_glu[li], b_glu[li]) * jax.nn.silu(z_ssm)
        y_pool = multiscale_pool(u_pool, w_pool[li], pool_scale[li]) * jax.nn.silu(z_pool)
        gates = jax.nn.sigmoid(g_logits).reshape(b, l, N_BRANCH, D_MODEL)
        merged = (gates[:, :, 0] * (y_att @ w_br_att[li])
                  + gates[:, :, 1] * (y_ssm @ w_br_ssm[li])
                  + gates[:, :, 2] * (y_pool @ w_br_pool[li]))
        x = x + gate[:, None, :] * (merged @ w_out[li])
    return rmsnorm(x, final_g)
```

```python
import math
import os
import types
from contextlib import ExitStack
import numpy as np
import concourse.bass as bass
import concourse.mybir as mybir
from concourse.bass_utils import run_bass_kernel_spmd

F32 = mybir.dt.float32
BF16 = mybir.dt.bfloat16
I32 = mybir.dt.int32
AF = mybir.ActivationFunctionType
ALU = mybir.AluOpType

D = 1024
SEQ = 4096
TT = 512
NT = SEQ // TT
NB = TT // 8
DEPTH = 2
NCORES = 8
EPS = 1e-6
POOL_W = (2, 4, 8, 16)
WIN_COLS = 512 + 384 + 512 + 1024 + 1024 + 8 * 384
ENGS = ("pe", "dve", "act", "pool", "sp")


class Prog:
    def __init__(self, nc, stack, same_engine_sync=True):
        self.nc = nc
        self.stack = stack
        self.q = {e: [] for e in ENGS}
        self.esem = {e: stack.enter_context(nc.semaphore("prog_" + e)) for e in ENGS if e != "sp"}
        self.cnt = {e: 0 for e in ENGS}
        self.waited = {e: {} for e in ENGS}
        self.last_w = {}
        self.readers = {}
        self.dsem = {}
        self.dval = {}
        self.same_engine_sync = same_engine_sync

    def _need(self, eng, tokens):
        out = {}
        for item in tokens:
            if item is None:
                continue
            if len(item) == 2:
                tok, kind = item
                if tok is None:
                    continue
            else:
                tok, kind = item, "raw"
            key, sem, val, src = tok
            if src == eng:
                if eng == "pe" or not self.same_engine_sync:
                    continue
                if kind != "raw" and os.environ.get("SES_RAW_ONLY", "1") == "1":
                    continue
            if self.waited[eng].get(key, 0) >= val:
                continue
            if key not in out or out[key][1] < val:
                out[key] = (sem, val)
        for key, (sem, val) in out.items():
            self.waited[eng][key] = val
            self.q[eng].append(lambda e, sem=sem, val=val: e.wait_ge(sem, val))

    def _deps(self, reads, writes):
        toks = []
        for b in reads:
            toks.append((self.last_w.get(b), "raw"))
        for b in writes:
            toks.append((self.last_w.get(b), "waw"))
            toks.extend((r_, "war") for r_ in self.readers.get(b, {}).values())
        return toks

    def _commit(self, tok, reads, writes):
        for b in reads:
            self.readers.setdefault(b, {})[tok[0]] = tok
        for b in writes:
            self.last_w[b] = tok
            self.readers[b] = {}

    @staticmethod
    def _snap(fn):
        if fn.__closure__ is None:
            return fn
        cells = []
        for c in fn.__closure__:
            try:
                cells.append(types.CellType(c.cell_contents))
            except ValueError:
                cells.append(c)
        return types.FunctionType(fn.__code__, fn.__globals__, fn.__name__, fn.__defaults__, tuple(cells))

    def op(self, eng, fn, reads=(), writes=()):
        fn = self._snap(fn)
        self._need(eng, self._deps(reads, writes))
        self.cnt[eng] += 1
        val = self.cnt[eng]
        sem = self.esem[eng]
        self.q[eng].append(lambda e, fn=fn, sem=sem: fn(e).then_inc(sem, 1))
        self._commit(("E" + eng, sem, val, eng), reads, writes)

    def dma(self, queue, out, in_, reads=(), writes=(), group=None):
        self._need(queue, self._deps(reads, writes))
        g = group if group is not None else (writes[0] if writes else reads[0])
        if g not in self.dsem:
            self.dsem[g] = self.stack.enter_context(self.nc.semaphore("dma_%d" % len(self.dsem)))
            self.dval[g] = 0
        self.dval[g] += 16
        sem, val = self.dsem[g], self.dval[g]
        self.q[queue].append(lambda e, out=out, in_=in_, sem=sem: e.dma_start(out=out, in_=in_).then_inc(sem, 16))
        self._commit(("D" + str(g), sem, val, "dma"), reads, writes)

    def barrier(self):
        toks = [("E" + x, self.esem[x], self.cnt[x], "bar") for x in self.esem if self.cnt[x] > 0]
        toks += [("D" + str(g), self.dsem[g], self.dval[g], "dma") for g in self.dsem]
        for e in ENGS:
            self._need(e, toks)

    def wait_all(self, eng, bufs):
        self._need(eng, [self.last_w.get(b) for b in bufs])

    def emit(self):
        with self.nc.Block() as block:
            @block.tensor
            def _(e):
                for f in self.q["pe"]:
                    f(e)

            @block.vector
            def _(e):
                for f in self.q["dve"]:
                    f(e)

            @block.scalar
            def _(e):
                for f in self.q["act"]:
                    f(e)

            @block.gpsimd
            def _(e):
                for f in self.q["pool"]:
                    f(e)

            @block.sync
            def _(e):
                for f in self.q["sp"]:
                    f(e)


def build_program(layers=(0, 1), final_norm=True, branches=(0, 1, 2), ntiles=NT, stage=9, debug=False):
    nc = bass.Bass("TRN2", target_bir_lowering=False)
    NL = len(layers)
    taps = {}

    def tap(name, ap, reads, cond=True):
        if not (debug and cond) or name in taps:
            return
        dt_ = ap.dtype
        d = nc.dram_tensor("dbg_" + name, list(ap.shape), dt_, kind="ExternalOutput").ap()
        taps[name] = d
        p.dma("sp", d, ap, reads=list(reads), writes=["dbg_" + name])

    def din(name, shape):
        return nc.dram_tensor(name, list(shape), F32, kind="ExternalInput").ap()

    x_d = din("x", [SEQ, D])
    out_d = nc.dram_tensor("out", [SEQ, D], F32, kind="ExternalOutput").ap()
    win_d = din("w_in_r", [DEPTH, 128, 8, WIN_COLS])
    wbr_d = din("w_br_r", [DEPTH, 8, 128, 1536])
    wout_d = din("w_out_r", [DEPTH, 128, 8, 1024])
    wglu_d = din("w_glu_r", [DEPTH, 128, 4, 512])
    wpool_d = din("w_pool_r", [DEPTH, 128, 4, 128])
    wada_d = din("w_ada_r", [DEPTH, 128, 8, 3072])
    colv_d = din("colvec", [DEPTH, 128, 44])
    bgate_d = din("bgate_rep", [DEPTH, 128, 1024])
    fing_d = din("finalg_rep", [128, 1024])
    cT_d = din("cT", [128, 8])
    ssmW_d = din("ssm_W", [DEPTH, 5, 128, 512])
    ssmV_d = din("ssm_V", [DEPTH, 7, 128, 512])
    ddiag_d = din("ddiag", [DEPTH, 128, 4, 128])
    ident_d = din("ident", [128, 128])
    bias_d = din("biasT", [128, 2, 8, 128])
    cf_d = din("pool_cf", [128, 4, 16])
    win_f, wbr_f, wout_f, wglu_f = win_d, wbr_d, wout_d, wglu_d
    win_d = nc.dram_tensor("win_b", [DEPTH, 128, 8, WIN_COLS], BF16, kind="Internal").ap()
    wbr_d = nc.dram_tensor("wbr_b", [DEPTH, 8, 128, 1536], BF16, kind="Internal").ap()
    wout_d = nc.dram_tensor("wout_b", [DEPTH, 128, 8, 1024], BF16, kind="Internal").ap()
    wglu_d = nc.dram_tensor("wglu_b", [DEPTH, 128, 4, 512], BF16, kind="Internal").ap()
    tab_d = nc.dram_tensor("ssm_tab", [DEPTH, 128, 21504], BF16, kind="Internal").ap()
    rot_d = nc.dram_tensor("ssm_rot", [DEPTH, 128, 2 * 16 * NB], F32, kind="Internal").ap()

    with ExitStack() as st:
        sb_bytes = [0]

        def sb(name, shape, dt=F32):
            n = 1
            for d_ in shape[1:]:
                n *= d_
            sb_bytes[0] += n * (2 if dt == BF16 else 4)
            return st.enter_context(nc.sbuf_tensor("s_" + name, list(shape), dt))

        p = Prog(nc, st, same_engine_sync=not os.environ.get("NOSES"))
        psum = st.enter_context(nc.psum_tensor("psum", [128, 7, 512], F32))
        psT = st.enter_context(nc.psum_tensor("psT", [128, 1024], BF16))
        bank_rr = [0]
        reserved = set()

        def bank():
            while True:
                b = bank_rr[0] % 7
                bank_rr[0] += 1
                if b not in reserved:
                    return b

        xs = sb("xs", [128, 4, D])
        hT = sb("hT", [128, 8, TT], BF16)
        NWB = 5
        wb = [sb("wb%d" % i, [128, 4096], BF16) for i in range(NWB)]
        wb_rr = [0]
        qz = sb("qz", [128, 8, TT], BF16)
        qT = qz[:, 0:4, :]
        zT = qz[:, 4:8, :]
        mrg = qz
        kkT = [sb("kkT%d" % l, [128, 2, 128 + TT], BF16) for l in range(NL)]
        vtok = [sb("vtok%d" % l, [128, 5, 128], BF16) for l in range(NL)]
        yT = [sb("yT%d" % i, [128, 4, TT], BF16) for i in range(3)]
        Eb = [sb("Eb%d" % i, [128, 512], BF16) for i in range(2)]
        gsb = [sb("gsb%d" % i, [128, 512]) for i in range(2)]
        uS = sb("uS", [128, 4, TT], BF16)
        uP = sb("uP", [128, 4, 16 + TT], BF16)
        uPh = [sb("uPh%d" % l, [128, 4, 16], BF16) for l in range(NL)]
        sA = sb("sA", [128, 16, 2, NB])
        sB = sb("sB", [128, 16, 2, NB])
        sT1 = sb("sT1", [128, 16, NB])
        sT2 = sb("sT2", [128, 16, NB])
        Xbf = sb("Xbf", [128, 16, 2, NB + 1], BF16)
        carry = [sb("carry%d" % l, [128, 16, 2, 1]) for l in range(NL)]
        rot = sb("rot", [128, 2, 16, NB])
        rcos = [rot[:, 0] for l in range(NL)]
        rsin = [rot[:, 1] for l in range(NL)]
        r8 = [sb("r8_%d" % l, [128, 16]) for l in range(NL)]
        tabs = sb("tabs", [128, 13312], BF16)
        BLv = tabs[:, 0:8192].rearrange("p (k i r m) -> p k i r m", k=4, i=8, r=2)
        CLv = tabs[:, 0:9216].rearrange("p (m r f) -> p m r f", m=9, r=2)
        KLv = tabs[:, 9216:13312].rearrange("p (k d m) -> p k d m", k=4, d=8)
        gate_rep = [sb("gate_rep%d" % l, [128, D]) for l in range(NL)]
        fing = sb("fing", [128, D])
        xo = sb("xo", [128, D])
        biasT = sb("biasT", [128, 2, 8, 128], BF16)
        ident = sb("ident", [128, 128], BF16)
        ones = sb("ones", [128, 64], BF16)
        cf = sb("cf", [128, 4, 16])
        Wlag = [sb("Wlag%d" % l, [128, 4, 128], BF16) for l in range(NL)]
        W0 = [sb("W0_%d" % l, [128, 4, 128], BF16) for l in range(NL)]
        colv = [sb("colv%d" % l, [128, 44]) for l in range(NL)]
        aT = [sb("aT%d" % l, [128, 8]) for l in range(NL)]
        shT = [sb("shT%d" % l, [128, 8]) for l in range(NL)]
        sinkexp = [sb("sinkexp%d" % l, [128, 4]) for l in range(NL)]
        ssq = sb("ssq", [128, 8])
        rstd = sb("rstd", [128, 8])
        xhat = sb("xhat", [128, D], BF16)
        junk = gsb[0][:].bitcast(BF16)
        dens2 = [sb("dens%d" % i, [128, 128]) for i in range(2)]
        ytmp2 = [sb("ytmp%d" % i, [128, 128]) for i in range(2)]
        ft1 = sb("ft1", [128, 512])
        ft2 = sb("ft2", [128, 512])
        ft3 = sb("ft3", [128, 512])
        pc1 = sb("pc1", [128, 16])
        epsc = sb("epsc", [128, 1])
        bbv_sb = sb("bbv", [128, 2, 512], BF16)

        p.dma("pool", ident[:], ident_d, writes=["ident"])
        p.dma("pool", biasT[:], bias_d, writes=["biasT"])
        for l in layers:
            for kc in range(8):
                p.dma("pool", win_d[l][:, kc, :], win_f[l][:, kc, :], writes=["cw"])
            for ft in range(8):
                p.dma("pool", wbr_d[l, ft], wbr_f[l, ft], writes=["cw"])
            p.dma("pool", wout_d[l], wout_f[l], writes=["cw"])
            p.dma("pool", wglu_d[l], wglu_f[l], writes=["cw"])
        p.dma("sp", cf[:], cf_d, writes=["cf"])
        p.op("dve", lambda e: e.memset(ones[:], 1.0), writes=["ones"])
        p.op("dve", lambda e: e.memset(epsc[:], EPS), writes=["epsc"])
        for li in range(NL):
            p.op("dve", lambda e, li=li: e.memset(carry[li][:], 0.0), writes=["carry%d" % li])
            p.op("dve", lambda e, li=li: e.memset(uPh[li][:], 0.0), writes=["uPh%d" % li])

        class _V:
            def __init__(self, ap):
                self.ap = ap
            def __getitem__(self, k):
                return self.ap[k]
        cTs = sb("cTs", [128, 8])
        cact = sb("cact", [128, 8])
        crep = _V(sA[:].rearrange("p a b c -> p (a b c)")[:, 0:1024].rearrange("p (k m) -> p k m", k=8))
        p.dma("sp", cTs[:], cT_d, writes=["cTs"])
        p.op("act", lambda e: e.activation(out=cact[:], in_=cTs[:], func=AF.Silu), reads=["cTs"], writes=["cact"])
        p.op("dve", lambda e: e.tensor_copy(out=crep[:], in_=cact[:].unsqueeze(2).to_broadcast([128, 8, 128])),
             reads=["cact"], writes=["crep"])
        xs_flat = xs[:].rearrange("p s d -> p (s d)")
        stg = [xs_flat[:, 0:2048].rearrange("p (k c) -> p k c", k=8), xs_flat[:, 2048:4096].rearrange("p (k c) -> p k c", k=8),
               wb[3][:].bitcast(F32).rearrange("p (k c) -> p k c", k=8), wb[4][:].bitcast(F32).rearrange("p (k c) -> p k c", k=8)]
        stg_rr = [0]
        wst = _V(stg[0])
        modT2 = [sb("modT%d" % li_, [128, 16]) for li_ in range(NL)]

        def ada_mm(li, l):
            p.dma("sp", colv[li][:], colv_d[l], writes=["colv%d" % li])
            p.dma("sp", gate_rep[li][:], bgate_d[l], writes=["gate_rep%d" % li])
            for n2 in range(12):
                si = stg_rr[0] % 4
                stg_rr[0] += 1
                sg, sk_ = stg[si], "wst%d" % si
                p.dma("sp", sg, wada_d[l][:, :, n2 * 256:(n2 + 1) * 256], writes=[sk_])
                if n2 < 8:
                    for j in range(2):
                        col = li * 16 + n2 * 2 + j
                        for kc in range(8):
                            p.op("pe", lambda e, j=j, kc=kc, col=col: e.matmul(
                                psum[:, 0, col:col + 1], lhsT=sg[:, kc, j * 128:(j + 1) * 128], rhs=cact[:, kc:kc + 1],
                                start=(kc == 0), stop=(kc == 7)), reads=[sk_, "cact"], writes=["ps0"])
                else:
                    bk_ = 1 + 2 * li + (n2 - 8) // 2
                    c0_ = ((n2 - 8) % 2) * 256
                    for kc in range(8):
                        p.op("pe", lambda e, kc=kc, bk_=bk_, c0_=c0_: e.matmul(
                            psum[:, bk_, c0_:c0_ + 256], lhsT=crep[:, kc, :], rhs=sg[:, kc, :], start=(kc == 0), stop=(kc == 7)),
                            reads=[sk_, "crep"], writes=["ps%d" % bk_])

        def ada_evac(li, l):
            modT = modT2[li]
            p.op("act", lambda e: e.activation(out=modT[:], in_=psum[:, 0, li * 16:(li + 1) * 16], func=AF.Copy),
                 reads=["ps0"], writes=["modT%d" % li])
            for h_ in range(2):
                p.op("dve", lambda e, h_=h_: e.tensor_tensor(
                    out=gate_rep[li][:, h_ * 512:(h_ + 1) * 512], in0=psum[:, 1 + 2 * li + h_, :],
                    in1=gate_rep[li][:, h_ * 512:(h_ + 1) * 512], op=ALU.add),
                    reads=["ps%d" % (1 + 2 * li + h_), "gate_rep%d" % li], writes=["gate_rep%d" % li])
            p.op("dve", lambda e, li=li: e.tensor_tensor(out=shT[li][:], in0=modT[:, 0:8], in1=colv[li][:, 8:16], op=ALU.add),
                 reads=["modT%d" % li, "colv%d" % li], writes=["shT%d" % li])
            p.op("dve", lambda e, li=li: e.tensor_tensor(out=aT[li][:], in0=modT[:, 8:16], in1=colv[li][:, 16:24], op=ALU.add),
                 reads=["modT%d" % li, "colv%d" % li], writes=["aT%d" % li])
            p.op("dve", lambda e, li=li: e.scalar_tensor_tensor(out=aT[li][:], in0=aT[li][:], scalar=1.0, in1=colv[li][:, 0:8],
                                                               op0=ALU.add, op1=ALU.mult),
                 reads=["aT%d" % li, "colv%d" % li], writes=["aT%d" % li])
            p.op("act", lambda e, li=li: e.activation(out=sinkexp[li][:], in_=colv[li][:, 40:44], func=AF.Exp),
                 reads=["colv%d" % li], writes=["sinkexp%d" % li])
            wps = fing[:, 0:512].rearrange("p (g o) -> p g o", g=4)
            p.dma("sp", wps, wpool_d[l], writes=["fingstage"])
            for gi, w in enumerate(POOL_W):
                p.op("act", lambda e, li=li, gi=gi, w=w: e.activation(out=Wlag[li][:, gi, :], in_=wps[:, gi, :], func=AF.Copy,
                                                                   scale=1.0 / w), reads=["fingstage"], writes=["Wlag%d" % li])
                p.op("act", lambda e, li=li, gi=gi, w=w: e.activation(out=W0[li][:, gi, :], in_=wps[:, gi, :], func=AF.Copy,
                                                                   scale=1.0 / w - 1.0), reads=["fingstage"], writes=["W0_%d" % li])

        if 1 in branches:
            NTMP = 20
            tm = []
            for i in range(12):
                tm.append(wb[i // 4][:].bitcast(F32)[:, (i % 4) * 512:(i % 4 + 1) * 512])
            for i in range(6):
                tm.append(yT[i // 2][:].rearrange("p a b -> p (a b)").bitcast(F32)[:, (i % 2) * 512:(i % 2 + 1) * 512])
            for i in range(2):
                tm.append(uS[:].rearrange("p a b -> p (a b)").bitcast(F32)[:, i * 512:(i + 1) * 512])
            tm = [_V(a) for a in tm]
            tki = _V(sB[:].rearrange("p a b c -> p (a b c)").bitcast(I32)[:, 0:512])
            TWO_PI = 2.0 * math.pi
            C1 = 6.28125
            C2 = TWO_PI - C1

            def tt(eng, o, a, b_, op, okey, akey, bkey):
                p.op(eng, lambda e: e.tensor_tensor(out=o, in0=a, in1=b_, op=op), reads=[akey, bkey], writes=[okey])

            def ts(eng, o, a, s1, s2, op0, op1, okey, akey):
                if s2 is None:
                    p.op(eng, lambda e: e.tensor_scalar(out=o, in0=a, scalar1=s1, scalar2=None, op0=op0), reads=[akey], writes=[okey])
                else:
                    p.op(eng, lambda e: e.tensor_scalar(out=o, in0=a, scalar1=s1, scalar2=s2, op0=op0, op1=op1), reads=[akey], writes=[okey])

            def T(i):
                return tm[i][:], "tm%d" % i

            def sincos(ang_i, sin_i, cos_i, t_a, t_b):
                a, ak = T(ang_i)
                ta, tak = T(t_a)
                tb, tbk = T(t_b)
                s_, sk = T(sin_i)
                c_, ck = T(cos_i)
                ts("dve", ta, a, 1.0 / TWO_PI, None, ALU.mult, None, tak, ak)
                p.op("dve", lambda e: e.tensor_copy(out=tki[:], in_=ta), reads=[tak], writes=["tki"])
                p.op("dve", lambda e: e.tensor_copy(out=ta, in_=tki[:]), reads=["tki"], writes=[tak])
                p.op("dve", lambda e: e.scalar_tensor_tensor(out=tb, in0=ta, scalar=-C1, in1=a, op0=ALU.mult, op1=ALU.add),
                     reads=[tak, ak], writes=[tbk])
                p.op("dve", lambda e: e.scalar_tensor_tensor(out=tb, in0=ta, scalar=-C2, in1=tb, op0=ALU.mult, op1=ALU.add),
                     reads=[tak, tbk], writes=[tbk])
                ts("dve", ta, tb, 0.5, math.pi / 2, ALU.mult, ALU.add, tak, tbk)
                p.op("act", lambda e: e.activation(out=s_, in_=tb, func=AF.Sin, scale=0.5), reads=[tbk], writes=[sk])
                p.op("act", lambda e: e.activation(out=c_, in_=ta, func=AF.Sin), reads=[tak], writes=[ck])
                tt("dve", ta, s_, c_, ALU.mult, tak, sk, ck)
                tt("dve", tb, s_, s_, ALU.mult, tbk, sk, sk)
                ts("dve", s_, ta, 2.0, None, ALU.mult, None, sk, tak)
                ts("dve", c_, tb, -2.0, 1.0, ALU.mult, ALU.add, ck, tbk)

            def cmul(or_i, oi_i, ar_i, ai_i, br_i, bi_i, t1_i, t2_i):
                o_r, ork = T(or_i); o_i, oik = T(oi_i)
                a_r, ark = T(ar_i); a_i, aik = T(ai_i)
                b_r, brk = T(br_i); b_i, bik = T(bi_i)
                t1, t1k = T(t1_i); t2, t2k = T(t2_i)
                tt("dve", t1, a_r, b_r, ALU.mult, t1k, ark, brk)
                tt("dve", t2, a_i, b_i, ALU.mult, t2k, aik, bik)
                tt("dve", t1, t1, t2, ALU.subtract, t1k, t1k, t2k)
                tt("dve", t2, a_r, b_i, ALU.mult, t2k, ark, bik)
                tt("dve", o_i, a_i, b_r, ALU.mult, oik, aik, brk)
                tt("dve", o_i, o_i, t2, ALU.add, oik, oik, t2k)
                p.op("dve", lambda e: e.tensor_copy(out=o_r, in_=t1), reads=[t1k], writes=[ork])

            def cmul2(or_i, oi_i, ar_i, ai_i, br_i, bi_i):
                o_r, ork = T(or_i); o_i, oik = T(oi_i)
                a_r, ark = T(ar_i); a_i, aik = T(ai_i)
                b_r, brk = T(br_i); b_i, bik = T(bi_i)
                t1, t1k = T(12); t2, t2k = T(13); t3, t3k = T(14); t4, t4k = T(15)
                tt("dve", t1, a_r, b_r, ALU.mult, t1k, ark, brk)
                tt("dve", t2, a_i, b_i, ALU.mult, t2k, aik, bik)
                tt("dve", t3, a_r, b_i, ALU.mult, t3k, ark, bik)
                tt("dve", t4, a_i, b_r, ALU.mult, t4k, aik, brk)
                tt("dve", o_r, t1, t2, ALU.subtract, ork, t1k, t2k)
                tt("dve", o_i, t3, t4, ALU.add, oik, t3k, t4k)

            def lam_base(src_d, l):
                for i in range(3):
                    p.dma("act", tm[i][:], src_d[l, i], writes=["tm%d" % i])
                p.op("act", lambda e: e.activation(out=tm[9][:], in_=tm[2][:], func=AF.Exp), reads=["tm2"], writes=["tm9"])
                tt("dve", tm[7][:], tm[0][:], tm[9][:], ALU.mult, "tm7", "tm0", "tm9")
                tt("dve", tm[8][:], tm[1][:], tm[9][:], ALU.mult, "tm8", "tm1", "tm9")
                sincos(8, 10, 11, 12, 13)
                p.op("act", lambda e: e.activation(out=tm[9][:], in_=tm[7][:], func=AF.Exp), reads=["tm7"], writes=["tm9"])
                tt("dve", tm[3][:], tm[9][:], tm[11][:], ALU.mult, "tm3", "tm9", "tm11")
                tt("dve", tm[4][:], tm[9][:], tm[10][:], ALU.mult, "tm4", "tm9", "tm10")
                tt("dve", tm[12][:], tm[0][:], tm[0][:], ALU.mult, "tm12", "tm0", "tm0")
                tt("dve", tm[13][:], tm[1][:], tm[1][:], ALU.mult, "tm13", "tm1", "tm1")
                tt("dve", tm[12][:], tm[12][:], tm[13][:], ALU.add, "tm12", "tm12", "tm13")
                p.op("dve", lambda e: e.reciprocal(out=tm[12][:], in_=tm[12][:]), reads=["tm12"], writes=["tm12"])
                ts("dve", tm[13][:], tm[3][:], -1.0, None, ALU.add, None, "tm13", "tm3")
                tt("dve", tm[5][:], tm[13][:], tm[0][:], ALU.mult, "tm5", "tm13", "tm0")
                tt("dve", tm[14][:], tm[4][:], tm[1][:], ALU.mult, "tm14", "tm4", "tm1")
                tt("dve", tm[5][:], tm[5][:], tm[14][:], ALU.add, "tm5", "tm5", "tm14")
                tt("dve", tm[6][:], tm[4][:], tm[0][:], ALU.mult, "tm6", "tm4", "tm0")
                tt("dve", tm[14][:], tm[13][:], tm[1][:], ALU.mult, "tm14", "tm13", "tm1")
                tt("dve", tm[6][:], tm[6][:], tm[14][:], ALU.subtract, "tm6", "tm6", "tm14")
                tt("dve", tm[5][:], tm[5][:], tm[12][:], ALU.mult, "tm5", "tm5", "tm12")
                tt("dve", tm[6][:], tm[6][:], tm[12][:], ALU.mult, "tm6", "tm6", "tm12")

            def s5_tables(li, l, part):
                if part == 0:
                    lam_base(ssmW_d, l)
                    p.dma("act", tm[9][:], ssmW_d[l, 3], writes=["tm9"])
                    p.dma("act", tm[10][:], ssmW_d[l, 4], writes=["tm10"])
                    cmul(9, 10, 9, 10, 5, 6, 12, 13)
                    cur, nxt = (9, 10), (18, 19)
                    for m in range(8):
                        i = 7 - m
                        p.op("act", lambda e, i=i, cur=cur: e.activation(out=BLv[:, :, i, 0, :], in_=tm[cur[0]][:].rearrange("p (k m) -> p k m", k=4),
                                                                func=AF.Copy), reads=["tm%d" % cur[0]], writes=["tabs"])
                        p.op("act", lambda e, i=i, cur=cur: e.activation(out=BLv[:, :, i, 1, :], in_=tm[cur[1]][:].rearrange("p (k m) -> p k m", k=4),
                                                                func=AF.Copy), reads=["tm%d" % cur[1]], writes=["tabs"])
                        if m < 7:
                            cmul2(nxt[0], nxt[1], cur[0], cur[1], 3, 4)
                            cur, nxt = nxt, cur
                    p.dma("sp", tab_d[l][:, 0:8192], tabs[:, 0:8192], reads=["tabs"], writes=["tab_d%d" % l])
                if part == 1:
                    lam_base(ssmV_d, l)
                    ts("dve", tm[15][:], tm[8][:], 8.0, None, ALU.mult, None, "tm15", "tm8")
                    sincos(15, 16, 17, 12, 13)
                    p.op("act", lambda e, li=li: e.activation(out=r8[li][:], in_=tm[7][:, 0:512:32], func=AF.Exp, scale=8.0),
                         reads=["tm7"], writes=["r8_%d" % li])
                    p.op("dve", lambda e, li=li: e.tensor_copy(out=rcos[li][:, :, 0], in_=tm[17][:, 0:512:32]), reads=["tm17"], writes=["rot"])
                    p.op("dve", lambda e, li=li: e.tensor_copy(out=rsin[li][:, :, 0], in_=tm[16][:, 0:512:32]), reads=["tm16"], writes=["rot"])
                    ur = sb("ur%d" % li, [128, 16]); ui = sb("ui%d" % li, [128, 16]); uq = sb("uq%d" % li, [128, 16]); uw = sb("uw%d" % li, [128, 16])
                    p.op("dve", lambda e, li=li: e.tensor_copy(out=ur[:], in_=tm[17][:, 0:512:32]), reads=["tm17"], writes=["ur%d" % li])
                    p.op("dve", lambda e, li=li: e.tensor_copy(out=ui[:], in_=tm[16][:, 0:512:32]), reads=["tm16"], writes=["ui%d" % li])
                    n = 1
                    while n < NB:
                        urb = ur[:].unsqueeze(2).to_broadcast([128, 16, n])
                        uib = ui[:].unsqueeze(2).to_broadcast([128, 16, n])
                        c0 = rcos[li][:, :, 0:n]; s0 = rsin[li][:, :, 0:n]
                        c1 = rcos[li][:, :, n:2 * n]; s1 = rsin[li][:, :, n:2 * n]
                        t1 = sT1[:, :, 0:n]; t2 = sT2[:, :, 0:n]
                        ck, sk = "rot", "rot"
                        uk = ["ur%d" % li, "ui%d" % li]
                        p.op("dve", lambda e, c0=c0, urb=urb, t1=t1: e.tensor_tensor(out=t1, in0=c0, in1=urb, op=ALU.mult), reads=[ck] + uk, writes=["sT1"])
                        p.op("dve", lambda e, s0=s0, uib=uib, t2=t2: e.tensor_tensor(out=t2, in0=s0, in1=uib, op=ALU.mult), reads=[sk] + uk, writes=["sT2"])
                        p.op("dve", lambda e, c1=c1, t1=t1, t2=t2: e.tensor_tensor(out=c1, in0=t1, in1=t2, op=ALU.subtract), reads=["sT1", "sT2"], writes=[ck])
                        p.op("dve", lambda e, c0=c0, uib=uib, t1=t1: e.tensor_tensor(out=t1, in0=c0, in1=uib, op=ALU.mult), reads=[ck] + uk, writes=["sT1"])
                        p.op("dve", lambda e, s0=s0, urb=urb, t2=t2: e.tensor_tensor(out=t2, in0=s0, in1=urb, op=ALU.mult), reads=[sk] + uk, writes=["sT2"])
                        p.op("dve", lambda e, s1=s1, t1=t1, t2=t2: e.tensor_tensor(out=s1, in0=t1, in1=t2, op=ALU.add), reads=["sT1", "sT2"], writes=[sk])
                        p.op("dve", lambda e: e.tensor_tensor(out=uq[:], in0=ur[:], in1=ur[:], op=ALU.mult), reads=uk, writes=["uq%d" % li])
                        p.op("dve", lambda e: e.tensor_tensor(out=uw[:], in0=ui[:], in1=ui[:], op=ALU.mult), reads=uk, writes=["uw%d" % li])
                        p.op("dve", lambda e: e.tensor_tensor(out=uq[:], in0=uq[:], in1=uw[:], op=ALU.subtract), reads=["uq%d" % li, "uw%d" % li], writes=["uq%d" % li])
                        p.op("dve", lambda e: e.scalar_tensor_tensor(out=ui[:], in0=ur[:], scalar=2.0, in1=ui[:], op0=ALU.mult, op1=ALU.mult),
                             reads=uk, writes=["ui%d" % li])
                        p.op("dve", lambda e: e.tensor_copy(out=ur[:], in_=uq[:]), reads=["uq%d" % li], writes=["ur%d" % li])
                        n *= 2
                    p.dma("act", tm[9][:], ssmV_d[l, 3], writes=["tm9"])
                    p.dma("act", tm[10][:], ssmV_d[l, 4], writes=["tm10"])
                    cmul(9, 10, 9, 10, 5, 6, 12, 13)
                    bbv = bbv_sb
                    p.op("act", lambda e: e.activation(out=bbv[:, 0, :], in_=tm[9][:], func=AF.Copy), reads=["tm9"], writes=["bbv"])
                    p.op("act", lambda e: e.activation(out=bbv[:, 1, :], in_=tm[10][:], func=AF.Copy), reads=["tm10"], writes=["bbv"])
                    p.dma("act", tm[9][:], ssmV_d[l, 5], writes=["tm9"])
                    p.dma("act", tm[10][:], ssmV_d[l, 6], writes=["tm10"])
                    cur, nxt = (9, 10), (18, 19)
                    for m in range(9):
                        p.op("act", lambda e, m=m, cur=cur: e.activation(out=CLv[:, m, 0, :], in_=tm[cur[0]][:], func=AF.Copy), reads=["tm%d" % cur[0]], writes=["tabs"])
                        p.op("act", lambda e, m=m, cur=cur: e.activation(out=CLv[:, m, 1, :], in_=tm[cur[1]][:], func=AF.Copy, scale=-1.0), reads=["tm%d" % cur[1]], writes=["tabs"])
                        if m < 8:
                            cmul2(nxt[0], nxt[1], cur[0], cur[1], 3, 4)
                            cur, nxt = nxt, cur
                    p.op("dve", lambda e: e.memset(KLv, 0.0), writes=["tabs"])
                    psK = psum[:, 5:7, :].rearrange("p b (k d m) -> p (b k) d m", k=2, d=8)
                    dds = fing[:, 512:1024].rearrange("p (k m) -> p k m", k=4)
                    p.dma("act", dds, ddiag_d[l], writes=["fingstage2"])
                    for pair in range(16):
                        k, j = pair // 4, pair % 4
                        for dl in range(8):
                            for ri in range(2):
                                p.op("pe", lambda e, pair=pair, k=k, j=j, dl=dl, ri=ri: e.matmul(
                                    psK[32 * j:32 * j + 32, k, dl, :], lhsT=bbv[:, ri, pair * 32:(pair + 1) * 32],
                                    rhs=CLv[:, dl, ri, pair * 32:(pair + 1) * 32], start=(ri == 0), stop=(ri == 1),
                                    tile_position=(0, 32 * j)),
                                    reads=["bbv", "tabs"], writes=["ps5", "ps6"])
                    for j in range(4):
                        p.op("dve", lambda e, j=j: e.tensor_copy(out=KLv[32 * j:32 * j + 32, :, :, 32 * j:32 * j + 32],
                                                                 in_=psK[32 * j:32 * j + 32, :, :, :]), reads=["ps5", "ps6"], writes=["tabs"])
                    p.op("dve", lambda e: e.tensor_tensor(out=ft1[:].rearrange("p (k m) -> p k m", k=4), in0=KLv[:, :, 0, :], in1=dds, op=ALU.add),
                         reads=["tabs", "fingstage2"], writes=["ft1"])
                    p.op("dve", lambda e: e.tensor_copy(out=KLv[:, :, 0, :], in_=ft1[:].rearrange("p (k m) -> p k m", k=4)), reads=["ft1"], writes=["tabs"])
                    p.dma("sp", tab_d[l][:, 8192:21504], tabs[:], reads=["tabs"], writes=["tab_d%d" % l])
                    p.dma("sp", rot_d[l], rot[:].rearrange("p a b c -> p (a b c)"), reads=["rot"], writes=["rot_d%d" % l])

        for li, l in enumerate(layers):
            ada_mm(li, l)
        for li, l in enumerate(layers):
            if 1 in branches:
                s5_tables(li, l, 0)
                s5_tables(li, l, 1)
        for li, l in enumerate(layers):
            ada_evac(li, l)
        if os.environ.get("SBDBG"):
            print("SBUF bytes/partition:", sb_bytes[0])
        p.barrier()
        def load_w(src_ap, ncols_total, shape_view=None):
            i = wb_rr[0] % NWB
            wb_rr[0] += 1
            key = "wb%d" % i
            dst = wb[i][:, 0:ncols_total]
            if shape_view is not None:
                dst = shape_view(dst)
            p.dma(os.environ.get("WQ", "pool"), dst, src_ap, reads=["cw"], writes=[key])
            if os.environ.get("SYNCW"):
                p.barrier()
            return wb[i], key

        def proj_fm(w_ap, wkey, ncol0, evac):
            b = bank()
            for kc in range(8):
                p.op("pe", lambda e, b=b, kc=kc: e.matmul(psum[:, b, :], lhsT=w_ap[:, kc, ncol0:ncol0 + 128], rhs=hT[:, kc, :],
                                                          start=(kc == 0), stop=(kc == 7)), reads=[wkey] + HTK, writes=["ps%d" % b])
            evac(b)

        v3 = lambda a: a.rearrange("p (k c) -> p k c", k=8)
        XK = ["xs0", "xs1", "xs2", "xs3"]
        HTK = ["hT0", "hT1", "hT2", "hT3"]
        pend_u = {}

        def norm_sub(li, s):
            p.op("act", lambda e: e.activation(out=junk, in_=xs[:, s, :], func=AF.Square, accum_out=ssq[:, s:s + 1]),
                 reads=[XK[s]], writes=["gsb0", "ssq%d" % s])
            p.op("act", lambda e: e.activation(out=rstd[:, s:s + 1], in_=ssq[:, s:s + 1], func=AF.Ln, scale=1.0 / D, bias=epsc[:, 0:1]),
                 reads=["ssq%d" % s, "epsc"], writes=["rstd%d" % s])
            p.op("act", lambda e: e.activation(out=rstd[:, s:s + 1], in_=rstd[:, s:s + 1], func=AF.Exp, scale=-0.5), reads=["rstd%d" % s], writes=["rstd%d" % s])
            p.op("dve", lambda e: e.tensor_scalar(out=xhat[:], in0=xs[:, s, :], scalar1=rstd[:, s:s + 1], scalar2=None, op0=ALU.mult),
                 reads=[XK[s], "rstd%d" % s], writes=["xhat"])
            for kc in range(8):
                p.op("pe", lambda e, kc=kc: e.transpose(out=psT[:, kc * 128:(kc + 1) * 128], in_=xhat[:, kc * 128:(kc + 1) * 128],
                                                      identity=ident[:]), reads=["xhat", "ident"], writes=["psT"])
            for kc in range(8):
                if kc % 2 == 0:
                    p.op("act", lambda e, kc=kc: e.activation(out=hT[:, kc, s * 128:(s + 1) * 128], in_=psT[:, kc * 128:(kc + 1) * 128],
                                                            func=AF.Identity, scale=aT[li][:, kc:kc + 1], bias=shT[li][:, kc:kc + 1]),
                         reads=["psT", "aT%d" % li, "shT%d" % li], writes=["hT%d" % s])
                else:
                    p.op("dve", lambda e, kc=kc: e.tensor_scalar(out=hT[:, kc, s * 128:(s + 1) * 128], in0=psT[:, kc * 128:(kc + 1) * 128],
                                                               scalar1=aT[li][:, kc:kc + 1], scalar2=shT[li][:, kc:kc + 1],
                                                               op0=ALU.mult, op1=ALU.add),
                         reads=["psT", "aT%d" % li, "shT%d" % li], writes=["hT%d" % s])

        def final_sub(t0, s):
            p.op("act", lambda e: e.activation(out=junk, in_=xs[:, s, :], func=AF.Square, accum_out=ssq[:, 4 + s:5 + s]),
                 reads=[XK[s]], writes=["gsb0", "ssqf%d" % s])
            p.op("act", lambda e: e.activation(out=rstd[:, 4 + s:5 + s], in_=ssq[:, 4 + s:5 + s], func=AF.Ln, scale=1.0 / D, bias=epsc[:, 0:1]),
                 reads=["ssqf%d" % s, "epsc"], writes=["rstdf%d" % s])
            p.op("act", lambda e: e.activation(out=rstd[:, 4 + s:5 + s], in_=rstd[:, 4 + s:5 + s], func=AF.Exp, scale=-0.5), reads=["rstdf%d" % s], writes=["rstdf%d" % s])
            p.op("dve", lambda e: e.scalar_tensor_tensor(out=xo[:], in0=xs[:, s, :], scalar=rstd[:, 4 + s:5 + s], in1=fing[:],
                                                       op0=ALU.mult, op1=ALU.mult), reads=[XK[s], "rstdf%d" % s, "fing"], writes=["xo"])

        def store_sub(t0, s):
            if final_norm:
                p.dma("sp", out_d[t0 + s * 128:t0 + (s + 1) * 128, :], xo[:], reads=["xo"], writes=["out%d" % s])
            else:
                p.dma("sp", out_d[t0 + s * 128:t0 + (s + 1) * 128, :], xs[:, s, :], reads=[XK[s]], writes=["out%d" % s])

        def load_x(t, s):
            t0_ = t * TT
            p.dma("sp", xs[:, s, :], x_d[t0_ + s * 128:t0_ + (s + 1) * 128, :], writes=[XK[s]], group="xin%d" % s)

        def att_scores(li, t, it):
            s, hp = it // 4, it % 4
            has_a = not (t == 0 and s == 0)
            kv = hp // 2
            Ei = it % 2
            abl = [0, 1] if has_a else [1]
            lo = 0 if has_a else 128
            b0 = 2 * (it % 2)
            bh = [b0, b0 + 1]
            for hh in range(2):
                h = 2 * hp + hh
                p.op("pe", lambda e, hh=hh, h=h: e.matmul(
                    psum[:, bh[hh], lo:256], lhsT=ident[:],
                    rhs=(biasT[:, :, h, :] if lo == 0 else biasT[:, 1, h, :]), start=True, stop=False),
                    reads=["ident", "biasT"], writes=["ps%d" % bh[hh]])
            for ab in abl:
                kcol = s * 128 + ab * 128
                for hh in range(2):
                    p.op("pe", lambda e, hh=hh, ab=ab, kcol=kcol: e.matmul(
                        psum[:, bh[hh], ab * 128:(ab + 1) * 128],
                        lhsT=kkT[li][hh * 64:(hh + 1) * 64, kv, kcol:kcol + 128],
                        rhs=qT[hh * 64:(hh + 1) * 64, hp, s * 128:(s + 1) * 128], start=False, stop=(ab == 1)),
                        reads=["kkT%d" % li, "qT"], writes=["ps%d" % bh[hh]])
            p.op("act", lambda e: e.activation(
                out=Eb[Ei][:].rearrange("p (h c) -> p h c", h=2)[:, :, lo:256], in_=psum[:, b0:b0 + 2, lo:256], func=AF.Exp, scale=0.125),
                reads=["ps%d" % b0, "ps%d" % (b0 + 1)], writes=["Eb%d" % Ei])
            return dict(s=s, hp=hp, kv=kv, Ei=Ei, abl=abl, it=it)

        def att_pv(li, c):
            s, hp, kv, Ei, abl = c["s"], c["hp"], c["kv"], c["Ei"], c["abl"]
            b2 = 4 + c["it"] % 3
            for which in range(2):
                for ai, ab in enumerate(abl):
                    for hh in range(2):
                        p.op("pe", lambda e, hh=hh, which=which, ab=ab, ai=ai: e.matmul(
                            psum[hh * 64:(hh + 1) * 64, b2, which * 128:(which + 1) * 128],
                            lhsT=(vtok[li][:, s + ab, kv * 64:(kv + 1) * 64] if which == 0 else ones[:]),
                            rhs=Eb[Ei][:, hh * 256 + ab * 128: hh * 256 + ab * 128 + 128],
                            start=(ai == 0), stop=(ai == len(abl) - 1)),
                            reads=["vtok%d" % li, "ones", "Eb%d" % Ei], writes=["ps%d" % b2])
            dn, yt = dens2[Ei], ytmp2[Ei]
            p.op("act", lambda e: e.activation(out=dn[:], in_=psum[:, b2, 128:256], func=AF.Ln, bias=sinkexp[li][:, hp:hp + 1]),
                 reads=["ps%d" % b2, "sinkexp%d" % li], writes=["dens%d" % Ei])
            p.op("act", lambda e: e.activation(out=dn[:], in_=dn[:], func=AF.Exp, scale=-1.0), reads=["dens%d" % Ei], writes=["dens%d" % Ei])
            p.op("dve", lambda e: e.tensor_tensor(out=yt[:], in0=psum[:, b2, 0:128], in1=dn[:], op=ALU.mult),
                 reads=["ps%d" % b2, "dens%d" % Ei], writes=["ytmp%d" % Ei])
            p.op("dve", lambda e: e.tensor_tensor(out=yT[0][:, hp, s * 128:(s + 1) * 128], in0=yt[:],
                                                in1=zT[:, hp, s * 128:(s + 1) * 128], op=ALU.mult),
                 reads=["ytmp%d" % Ei, "zT"], writes=["yT0"])

        def attention(li, l, t):
            wq, kq = load_w(win_d[l][:, :, 0:512], 4096, v3)
            wq3 = v3(wq[:, 0:4096])
            for ft in range(4):
                proj_fm(wq3, kq, ft * 128, lambda b, ft=ft: p.op("act", lambda e: e.activation(
                    out=qT[:, ft, :], in_=psum[:, b, :], func=AF.Copy), reads=["ps%d" % b], writes=["qT"]))
            wk, kk_ = load_w(win_d[l][:, :, 512:896], 3072, v3)
            wk3 = v3(wk[:, 0:3072])
            for kv in range(2):
                proj_fm(wk3, kk_, kv * 128, lambda b, kv=kv: p.op("dve", lambda e: e.tensor_copy(
                    out=kkT[li][:, kv, 128:128 + TT], in_=psum[:, b, :]), reads=["ps%d" % b], writes=["kkT%d" % li]))
            for s in range(4):
                b = bank()
                for kc in range(8):
                    p.op("pe", lambda e, kc=kc: e.matmul(psum[:, b, 0:128], lhsT=hT[:, kc, s * 128:(s + 1) * 128],
                                                       rhs=wk3[:, kc, 256:384], start=(kc == 0), stop=(kc == 7)),
                         reads=[kk_, "hT%d" % s], writes=["ps%d" % b])
                p.op("dve", lambda e: e.tensor_copy(out=vtok[li][:, s + 1, :], in_=psum[:, b, 0:128]),
                     reads=["ps%d" % b], writes=["vtok%d" % li])
            wz, kz = load_w(win_d[l][:, :, 896:1408], 4096, v3)
            wz3 = v3(wz[:, 0:4096])
            for ft in range(4):
                proj_fm(wz3, kz, ft * 128, lambda b, ft=ft: p.op("act", lambda e: e.activation(
                    out=zT[:, ft, :], in_=psum[:, b, :], func=AF.Silu), reads=["ps%d" % b], writes=["zT"]))
            c0_ = (t == 0 and li == 0)
            tap("qT", qT, ["qT"], c0_)
            tap("kkT", kkT[li][:], ["kkT%d" % li], c0_)
            tap("vtok", vtok[li][:], ["vtok%d" % li], c0_)
            tap("zT_att", zT, ["zT"], c0_)
            prev = None
            for it in range(16):
                cur = att_scores(li, t, it)
                if prev is not None:
                    att_pv(li, prev)
                prev = cur
            att_pv(li, prev)
            tap("yT0", yT[0][:], ["yT0"], c0_)
            p.op("act", lambda e: e.activation(out=kkT[li][:, :, 0:128], in_=kkT[li][:, :, TT:TT + 128], func=AF.Copy),
                 reads=["kkT%d" % li], writes=["kkT%d" % li])
            p.op("act", lambda e: e.activation(out=vtok[li][:, 0, :], in_=vtok[li][:, 4, :], func=AF.Copy),
                 reads=["vtok%d" % li], writes=["vtok%d" % li])

        pskeys = ["ps3", "ps4", "ps5", "ps6"]
        psD5 = psum[:, 3:7, :].rearrange("p j (k r n) -> p j k r n", k=4, r=2)
        psDv = [psD5[:, :, :, ri_, :].rearrange("p j k n -> p k j n") for ri_ in range(2)]
        v4 = lambda a: a.rearrange("p (k j) n -> p k j n", k=4)

        def uproj_begin(l):
            wu, ku = load_w(win_d[l][:, :, 1408:1920], 4096, v3)
            reserved.update({3, 4, 5, 6})
            return dict(w3=v3(wu[:, 0:4096]), key=ku)

        def uproj_sub(ctx, s):
            w3, ku = ctx["w3"], ctx["key"]
            for ft in range(4):
                for kc in range(8):
                    p.op("pe", lambda e, ft=ft, kc=kc: e.matmul(
                        psum[:, 3 + ft, s * 128:(s + 1) * 128], lhsT=w3[:, kc, ft * 128:(ft + 1) * 128], rhs=hT[:, kc, s * 128:(s + 1) * 128],
                        start=(kc == 0), stop=(kc == 7)), reads=[ku, "hT%d" % s], writes=["ps%d" % (3 + ft)])

        def uproj_end(ctx):
            for ft in range(4):
                p.op("act", lambda e, ft=ft: e.activation(out=uS[:, ft, :], in_=psum[:, 3 + ft, :], func=AF.Copy),
                     reads=["ps%d" % (3 + ft)], writes=["uS"])
            reserved.clear()

        def ssm_part1(li, l):
            ctx = pend_u.pop("ctx", None)
            if ctx is None:
                ctx = uproj_begin(l)
                for s in range(4):
                    uproj_sub(ctx, s)
            uproj_end(ctx)
            p.dma("sp", tabs[:, 0:8192], tab_d[l][:, 0:8192], reads=["tab_d%d" % l], writes=["tabs"])
            p.dma("sp", rot[:].rearrange("p a b c -> p (a b c)"), rot_d[l], reads=["rot_d%d" % l], writes=["rot"])
            for k in range(4):
                for ri in range(2):
                    for i in range(8):
                        for j in range(4):
                            p.op("pe", lambda e, k=k, j=j, ri=ri, i=i: e.matmul(
                                psD5[:, j, k, ri, :], lhsT=BLv[32 * j:32 * j + 32, k, i, ri, :], rhs=uS[32 * j:32 * j + 32, k, i:TT:8],
                                start=(i == 0), stop=(i == 7), tile_position=(32 * j, 0)),
                                reads=["tabs", "uS"], writes=[pskeys[j]])
            p.dma("sp", tabs[:], tab_d[l][:, 8192:21504], reads=["tab_d%d" % l], writes=["tabs"])
            rc, rs = v4(rcos[li]), v4(rsin[li])
            T1, T2 = v4(sT1[:]), v4(sT2[:])
            p.op("dve", lambda e: e.tensor_tensor(out=T1, in0=psDv[0], in1=rc, op=ALU.mult), reads=pskeys + ["rot"], writes=["sT1"])
            p.op("dve", lambda e: e.tensor_tensor(out=T2, in0=psDv[1], in1=rs, op=ALU.mult), reads=pskeys + ["rot"], writes=["sT2"])
            p.op("dve", lambda e: e.tensor_tensor(out=sA[:, :, 0, :], in0=sT1[:], in1=sT2[:], op=ALU.add), reads=["sT1", "sT2"], writes=["sA"])
            p.op("dve", lambda e: e.tensor_tensor(out=T1, in0=psDv[1], in1=rc, op=ALU.mult), reads=pskeys + ["rot"], writes=["sT1"])
            p.op("dve", lambda e: e.tensor_tensor(out=T2, in0=psDv[0], in1=rs, op=ALU.mult), reads=pskeys + ["rot"], writes=["sT2"])
            p.op("dve", lambda e: e.tensor_tensor(out=sA[:, :, 1, :], in0=sT1[:], in1=sT2[:], op=ALU.subtract), reads=["sT1", "sT2"], writes=["sA"])
            for pair in range(16):
                for ri in range(2):
                    p.op("dve", lambda e, pair=pair, ri=ri: e.tensor_tensor_scan(
                        out=sB[:, pair, ri, :], data0=r8[li][:, pair:pair + 1].to_broadcast([128, NB]), data1=sA[:, pair, ri, :],
                        initial=carry[li][:, pair, ri, :], op0=ALU.mult, op1=ALU.add),
                        reads=["sA", "r8_%d" % li, "carry%d" % li], writes=["sB"])
            p.op("dve", lambda e: e.tensor_tensor(out=sT1[:], in0=sB[:, :, 0, :], in1=rcos[li][:], op=ALU.mult), reads=["sB", "rot"], writes=["sT1"])
            p.op("dve", lambda e: e.tensor_tensor(out=sT2[:], in0=sB[:, :, 1, :], in1=rsin[li][:], op=ALU.mult), reads=["sB", "rot"], writes=["sT2"])
            p.op("dve", lambda e: e.tensor_tensor(out=sA[:, :, 0, :], in0=sT1[:], in1=sT2[:], op=ALU.subtract), reads=["sT1", "sT2"], writes=["sA"])
            p.op("dve", lambda e: e.tensor_tensor(out=sT1[:], in0=sB[:, :, 0, :], in1=rsin[li][:], op=ALU.mult), reads=["sB", "rot"], writes=["sT1"])
            p.op("dve", lambda e: e.tensor_tensor(out=sT2[:], in0=sB[:, :, 1, :], in1=rcos[li][:], op=ALU.mult), reads=["sB", "rot"], writes=["sT2"])
            p.op("dve", lambda e: e.tensor_tensor(out=sA[:, :, 1, :], in0=sT1[:], in1=sT2[:], op=ALU.add), reads=["sT1", "sT2"], writes=["sA"])
            p.op("act", lambda e: e.activation(out=Xbf[:, :, :, 0:1], in_=carry[li][:], func=AF.Copy), reads=["carry%d" % li], writes=["Xbf"])
            p.op("act", lambda e: e.activation(out=Xbf[:, :, :, 1:NB + 1], in_=sA[:], func=AF.Copy), reads=["sA"], writes=["Xbf"])
            p.op("dve", lambda e: e.tensor_copy(out=carry[li][:], in_=sA[:, :, :, NB - 1:NB]), reads=["sA", "Xbf"], writes=["carry%d" % li])

        def ssm_part2(li, l):
            wz, kz = load_w(win_d[l][:, :, 1920:2432], 4096, v3)
            wz3 = v3(wz[:, 0:4096])
            for ft in range(4):
                proj_fm(wz3, kz, ft * 128, lambda b, ft=ft: p.op("act", lambda e: e.activation(
                    out=zT[:, ft, :], in_=psum[:, b, :], func=AF.Silu), reads=["ps%d" % b], writes=["zT"]))
            wg, kg = load_w(wglu_d[l], 2048, lambda a: a.rearrange("p (k c) -> p k c", k=4))
            wg3 = wg[:, 0:2048].rearrange("p (k c) -> p k c", k=4)
            for k in range(4):
                bk = 3 + k
                for dl in range(8):
                    p.op("pe", lambda e, k=k, bk=bk, dl=dl: e.matmul(
                        psum[:, bk, :].rearrange("p (n i) -> p n i", i=8)[:, :, dl:8],
                        lhsT=KLv[:, k, dl, :],
                        rhs=uS[:, k, :].rearrange("p (n i) -> p n i", i=8)[:, :, 0:8 - dl],
                        start=(dl == 0), stop=False), reads=["tabs", "uS"], writes=["ps%d" % bk])
                for i in range(8):
                    for ri in range(2):
                        for j in range(4):
                            pair = 4 * k + j
                            last = (j == 3 and i == 7 and ri == 1)
                            p.op("pe", lambda e, k=k, bk=bk, j=j, pair=pair, i=i, ri=ri, last=last: e.matmul(
                                psum[32 * j:32 * j + 32, bk, i:TT:8], lhsT=CLv[:, i + 1, ri, pair * 32:(pair + 1) * 32],
                                rhs=Xbf[:, pair, ri, 0:NB], start=False, stop=last, tile_position=(0, 32 * j)),
                                reads=["tabs", "Xbf"], writes=["ps%d" % bk])
                p.op("act", lambda e, bk=bk: e.activation(out=ft1[:], in_=psum[:, bk, :], func=AF.Square), reads=["ps%d" % bk], writes=["ft1"])
                p.op("dve", lambda e: e.tensor_scalar(out=ft1[:], in0=ft1[:], scalar1=0.044715, scalar2=1.0, op0=ALU.mult, op1=ALU.add), reads=["ft1"], writes=["ft1"])
                p.op("dve", lambda e, bk=bk: e.tensor_tensor(out=ft1[:], in0=psum[:, bk, :], in1=ft1[:], op=ALU.mult), reads=["ps%d" % bk, "ft1"], writes=["ft1"])
                p.op("act", lambda e: e.activation(out=ft1[:], in_=ft1[:], func=AF.Sigmoid, scale=1.5957691216057308), reads=["ft1"], writes=["ft1"])
                p.op("dve", lambda e, bk=bk: e.tensor_tensor(out=ft2[:], in0=psum[:, bk, :], in1=ft1[:], op=ALU.mult), reads=["ps%d" % bk, "ft1"], writes=["ft2"])
                p.op("act", lambda e, k=k: e.activation(out=uP[:, k, 16:16 + TT], in_=ft2[:], func=AF.Copy), reads=["ft2"], writes=["uP"])
                p.op("dve", lambda e, k=k: e.tensor_tensor(out=qT[:, k, :], in0=ft2[:], in1=zT[:, k, :], op=ALU.mult), reads=["ft2", "zT"], writes=["qT"])
            return wg3, kg

        def ssm_part2b(li, l, wg3, kg):
            for ft in range(4):
                b = bank()
                for kc in range(4):
                    p.op("pe", lambda e, kc=kc, ft=ft: e.matmul(psum[:, b, :], lhsT=wg3[:, kc, ft * 128:(ft + 1) * 128], rhs=uP[:, kc, 16:16 + TT],
                                                              start=(kc == 0), stop=(kc == 3)), reads=[kg, "uP"], writes=["ps%d" % b])
                p.op("act", lambda e, ft=ft: e.activation(out=ft3[:], in_=psum[:, b, :], func=AF.Sigmoid, bias=colv[li][:, 36 + ft:37 + ft]),
                     reads=["ps%d" % b, "colv%d" % li], writes=["ft3"])
                p.op("dve", lambda e, ft=ft: e.tensor_tensor(out=yT[1][:, ft, :], in0=ft3[:], in1=qT[:, ft, :], op=ALU.mult), reads=["ft3", "qT"], writes=["yT1"])

        def pooling(li, l, t):
            wu, ku = load_w(win_d[l][:, :, 2432:2944], 4096, v3)
            wu3 = v3(wu[:, 0:4096])
            p.op("act", lambda e: e.activation(out=uP[:, :, 0:16], in_=uPh[li][:], func=AF.Copy), reads=["uPh%d" % li], writes=["uP"])
            for ft in range(4):
                proj_fm(wu3, ku, ft * 128, lambda b, ft=ft: p.op("act", lambda e: e.activation(
                    out=uP[:, ft, 16:16 + TT], in_=psum[:, b, :], func=AF.Copy), reads=["ps%d" % b], writes=["uP"]))
            p.op("act", lambda e: e.activation(out=uPh[li][:], in_=uP[:, :, TT:TT + 16], func=AF.Copy), reads=["uP"], writes=["uPh%d" % li])
            wz, kz = load_w(win_d[l][:, :, 2944:3456], 4096, v3)
            wz3 = v3(wz[:, 0:4096])
            for ft in range(4):
                proj_fm(wz3, kz, ft * 128, lambda b, ft=ft: p.op("act", lambda e: e.activation(
                    out=zT[:, ft, :], in_=psum[:, b, :], func=AF.Silu), reads=["ps%d" % b], writes=["zT"]))
            for gi, w in enumerate(POOL_W):
                b = bank()
                for kk in range(w):
                    p.op("pe", lambda e, gi=gi, kk=kk: e.matmul(
                        psum[:, b, :], lhsT=(W0[li][:, gi, :] if kk == 0 else Wlag[li][:, gi, :]), rhs=uP[:, gi, 16 - kk:16 - kk + TT],
                        start=(kk == 0), stop=(kk == w - 1)), reads=["W0_%d" % li, "Wlag%d" % li, "uP"], writes=["ps%d" % b])
                c0 = 0
                if t == 0:
                    c0 = 16
                    b2 = bank()
                    for kk in range(w):
                        p.op("pe", lambda e, gi=gi, kk=kk: e.matmul(
                            psum[:, b2, 0:16], lhsT=Wlag[li][:, gi, :], rhs=uP[:, gi, 16 - kk:32 - kk],
                            start=(kk == 0), stop=(kk == w - 1)), reads=["Wlag%d" % li, "uP"], writes=["ps%d" % b2])
                    p.op("dve", lambda e, gi=gi: e.tensor_tensor(out=pc1[:], in0=psum[:, b2, 0:16], in1=cf[:, gi, :], op=ALU.mult),
                         reads=["ps%d" % b2, "cf"], writes=["pc1"])
                    p.op("dve", lambda e: e.tensor_tensor(out=pc1[:], in0=psum[:, b, 0:16], in1=pc1[:], op=ALU.add),
                         reads=["ps%d" % b, "pc1"], writes=["pc1"])
                    p.op("dve", lambda e, gi=gi: e.scalar_tensor_tensor(out=yT[2][:, gi, 0:16], in0=pc1[:], scalar=colv[li][:, 32 + gi:33 + gi],
                                                                      in1=zT[:, gi, 0:16], op0=ALU.mult, op1=ALU.mult),
                         reads=["pc1", "colv%d" % li, "zT"], writes=["yT2"])
                p.op("dve", lambda e, gi=gi, c0=c0: e.scalar_tensor_tensor(
                    out=yT[2][:, gi, c0:TT], in0=psum[:, b, c0:TT], scalar=colv[li][:, 32 + gi:33 + gi], in1=zT[:, gi, c0:TT],
                    op0=ALU.mult, op1=ALU.mult), reads=["ps%d" % b, "colv%d" % li, "zT"], writes=["yT2"])

        def merge(li, l, t, hook=None):
            order = [b_ for b_ in (0, 2, 1) if b_ in branches]
            c0_ = (t == 0 and li == 0)
            for ft in range(8):
                wgt, kgt = load_w(win_d[l][:, :, 3456 + ft * 384: 3456 + (ft + 1) * 384], 3072, v3)
                wgt3 = v3(wgt[:, 0:3072])
                wbr, kbr = load_w(wbr_d[l, ft], 1536)
                first = True
                for bi_, br in enumerate(order):
                    if ft == 0 and br == 1 and hook is not None:
                        hook()
                    bg = bank()
                    for kc in range(8):
                        p.op("pe", lambda e, kc=kc, br=br: e.matmul(
                            psum[:, bg, :], lhsT=wgt3[:, kc, br * 128:(br + 1) * 128], rhs=hT[:, kc, :], start=(kc == 0), stop=(kc == 7)),
                            reads=[kgt] + HTK, writes=["ps%d" % bg])
                    gi_ = (ft * 3 + bi_) % 2
                    p.op("act", lambda e: e.activation(out=gsb[gi_][:], in_=psum[:, bg, :], func=AF.Sigmoid),
                         reads=["ps%d" % bg], writes=["gsb%d" % gi_])
                    bb = bank()
                    for kc in range(4):
                        p.op("pe", lambda e, kc=kc, br=br: e.matmul(
                            psum[:, bb, :], lhsT=wbr[:, (br * 4 + kc) * 128:(br * 4 + kc + 1) * 128],
                            rhs=yT[br][:, kc, :], start=(kc == 0), stop=(kc == 3)), reads=[kbr, "yT%d" % br], writes=["ps%d" % bb])
                    if first:
                        p.op("dve", lambda e: e.tensor_tensor(out=ft1[:], in0=psum[:, bb, :], in1=gsb[gi_][:], op=ALU.mult),
                             reads=["ps%d" % bb, "gsb%d" % gi_], writes=["ft1"])
                        first = False
                    else:
                        p.op("dve", lambda e: e.tensor_tensor(out=ft2[:], in0=psum[:, bb, :], in1=gsb[gi_][:], op=ALU.mult),
                             reads=["ps%d" % bb, "gsb%d" % gi_], writes=["ft2"])
                        p.op("dve", lambda e: e.tensor_tensor(out=ft1[:], in0=ft1[:], in1=ft2[:], op=ALU.add), reads=["ft1", "ft2"], writes=["ft1"])
                p.op("act", lambda e, ft=ft: e.activation(out=mrg[:, ft, :], in_=ft1[:], func=AF.Copy), reads=["ft1"], writes=["qT", "zT"])
            tap("yT1", yT[1][:], ["yT1"], c0_ and 1 in branches)
            tap("yT2", yT[2][:], ["yT2"], c0_ and 2 in branches)
            tap("mrg", mrg[:], ["qT", "zT"], c0_)

        def outproj(li, l, t, t0):
            ws = []
            for half in range(2):
                wo, ko = load_w(wout_d[l][:, half * 4:(half + 1) * 4, :], 4096, lambda a: a.rearrange("p (k c) -> p k c", k=4))
                wo3_ = wo[:, 0:4096].rearrange("p (k c) -> p k c", k=4)
                ws.append((wo3_, ko))
            uctx = None
            if 1 in branches and (li + 1 < NL or t + 1 < ntiles):
                uctx = uproj_begin(layers[li + 1] if li + 1 < NL else layers[0])
                pend_u["ctx"] = uctx
            for s in range(4):
                for dh in range(2):
                    b = bank()
                    for ft in range(8):
                        wsrc, wkey = ws[ft // 4]
                        p.op("pe", lambda e, ft=ft, dh=dh, wsrc=wsrc: e.matmul(
                            psum[:, b, :], lhsT=mrg[:, ft, s * 128:(s + 1) * 128], rhs=wsrc[:, ft % 4, dh * 512:(dh + 1) * 512],
                            start=(ft == 0), stop=(ft == 7)), reads=["qT", "zT", wkey], writes=["ps%d" % b])
                    p.op("dve", lambda e, dh=dh: e.tensor_tensor(out=ft3[:], in0=psum[:, b, :], in1=gate_rep[li][:, dh * 512:(dh + 1) * 512], op=ALU.mult),
                         reads=["ps%d" % b, "gate_rep%d" % li], writes=["ft3"])
                    p.op("dve", lambda e, dh=dh: e.tensor_tensor(out=xs[:, s, dh * 512:(dh + 1) * 512], in0=xs[:, s, dh * 512:(dh + 1) * 512], in1=ft3[:], op=ALU.add),
                         reads=[XK[s], "ft3"], writes=[XK[s]])
                if li + 1 < NL:
                    if s >= 1:
                        norm_sub(li + 1, s - 1)
                    if s >= 2 and uctx is not None:
                        uproj_sub(uctx, s - 2)
                else:
                    if s >= 1:
                        post_final(t, t0, s - 1)
                    if s >= 2 and t + 1 < ntiles:
                        norm_sub(0, s - 2)
                    if s >= 3 and t + 1 < ntiles and uctx is not None:
                        uproj_sub(uctx, s - 3)
            if li + 1 < NL:
                norm_sub(li + 1, 3)
                if uctx is not None:
                    uproj_sub(uctx, 2)
                    uproj_sub(uctx, 3)
            else:
                post_final(t, t0, 3)
                if t + 1 < ntiles:
                    norm_sub(0, 2)
                    if uctx is not None:
                        uproj_sub(uctx, 1)
                    norm_sub(0, 3)
                    if uctx is not None:
                        uproj_sub(uctx, 2)
                        uproj_sub(uctx, 3)

        def post_final(t, t0, s):
            if final_norm:
                final_sub(t0, s)
            store_sub(t0, s)
            if t + 1 < ntiles:
                load_x(t + 1, s)

        for t in range(ntiles):
            t0 = t * TT
            if t == 0:
                for s in range(4):
                    load_x(0, s)
                if final_norm:
                    p.dma("sp", fing[:], fing_d, writes=["fing"])
                for s in range(4):
                    norm_sub(0, s)
            for li, l in enumerate(layers):
                c0_ = (t == 0 and li == 0)
                tap("hT", hT[:], HTK, c0_)
                tap("aT", aT[li][:], ["aT%d" % li], c0_)
                tap("shT", shT[li][:], ["shT%d" % li], c0_)
                tap("gate_rep", gate_rep[li][:], ["gate_rep%d" % li], c0_)
                tap("sinkexp", sinkexp[li][:], ["sinkexp%d" % li], c0_)
                if 1 in branches:
                    ssm_part1(li, l)
                if 2 in branches:
                    pooling(li, l, t)
                if 0 in branches:
                    attention(li, l, t)
                hook = None
                if 1 in branches:
                    wg3_, kg_ = ssm_part2(li, l)
                    hook = (lambda li=li, l=l, wg3_=wg3_, kg_=kg_: ssm_part2b(li, l, wg3_, kg_))
                merge(li, l, t, hook)
                outproj(li, l, t, t0)
        p.wait_all("sp", ["out%d" % s_ for s_ in range(4)] + ["dbg_" + n for n in taps])
        p.emit()
    return nc


def _const_tables():
    ident = np.eye(128, dtype=np.float32)
    BIG = -30000.0
    bias = np.zeros((128, 2, 8, 128), np.float32)
    j = np.arange(128)[:, None]
    i = np.arange(128)[None, :]
    ck, cq = j // 64, i // 64
    for h in range(8):
        slope = 2.0 ** (-(h + 1))
        da = (128 + i - j).astype(np.float32)
        a = -8.0 * slope * da
        a = np.where(ck >= cq, a, BIG)
        db = np.abs(i - j).astype(np.float32)
        b = -8.0 * slope * db
        b = np.where(ck <= cq, b, BIG)
        bias[:, 0, h, :] = a
        bias[:, 1, h, :] = b
    cf = np.zeros((128, 4, 16), np.float32)
    tt_ = np.arange(16)
    for gi, w in enumerate(POOL_W):
        cf[:, gi, :] = (w / np.minimum(tt_ + 1, w) - 1.0)[None, :]
    return ident, bias, cf


def _win_perm():
    cols = list(range(0, 512))
    cols += list(range(512, 576)) * 2 + list(range(576, 640)) * 2 + list(range(640, 768))
    cols += list(range(1792, 2304))
    cols += list(range(768, 1280)) + list(range(2304, 2816))
    cols += list(range(1280, 1792)) + list(range(2816, 3328))
    for ft in range(8):
        for br in range(3):
            cols += list(range(3328 + br * 1024 + ft * 128, 3328 + br * 1024 + (ft + 1) * 128))
    assert len(cols) == WIN_COLS
    return np.asarray(cols)


def _shared_layout(inp):
    f = lambda a: np.ascontiguousarray(a, dtype=np.float32)
    perm = _win_perm()
    w_in = inp["w_in"]
    w_in_r = f(w_in[:, :, perm].reshape(DEPTH, 8, 128, WIN_COLS).transpose(0, 2, 1, 3))
    wbr = np.stack([inp["w_br_att"], inp["w_br_ssm"], inp["w_br_pool"]], axis=1)
    w_br_r = f(wbr.reshape(DEPTH, 3, 4, 128, 8, 128).transpose(0, 4, 3, 1, 2, 5).reshape(DEPTH, 8, 128, 1536))
    w_out_r = f(inp["w_out"].reshape(DEPTH, 8, 128, 1024).transpose(0, 2, 1, 3))
    w_glu_r = f(inp["w_glu"].reshape(DEPTH, 4, 128, 512).transpose(0, 2, 1, 3))
    w_pool_r = f(inp["w_pool"].transpose(0, 2, 1, 3))
    w_ada_r = f(inp["w_ada"].reshape(DEPTH, 8, 128, 3072).transpose(0, 2, 1, 3))
    colvec = np.zeros((DEPTH, 128, 44), np.float32)
    colvec[:, :, 0:8] = inp["norm_g"].reshape(DEPTH, 8, 128).transpose(0, 2, 1)
    colvec[:, :, 8:32] = inp["b_ada"].reshape(DEPTH, 24, 128).transpose(0, 2, 1)
    colvec[:, :, 32:36] = inp["pool_scale"].reshape(DEPTH, 4, 128).transpose(0, 2, 1)
    colvec[:, :, 36:40] = inp["b_glu"].reshape(DEPTH, 4, 128).transpose(0, 2, 1)
    sk = inp["attn_sinks"].reshape(DEPTH, 4, 2)
    colvec[:, :, 40:44] = np.repeat(sk.transpose(0, 2, 1), 64, axis=1)
    bgate_rep = f(np.broadcast_to(inp["b_ada"][:, None, 2048:3072], (DEPTH, 128, 1024)))
    finalg_rep = f(np.broadcast_to(inp["final_g"][None, :], (128, 1024)))
    are, aim, ldt = inp["ssm_a_re"], inp["ssm_a_im"], inp["ssm_log_dt"]
    Bre, Bim, Cre, Cim = inp["ssm_b_re"], inp["ssm_b_im"], inp["ssm_c_re"], inp["ssm_c_im"]
    q = np.arange(128)
    fidx = np.arange(512)
    jq, g2q, cq_ = q // 32, (q // 16) % 2, q % 16
    kf, g2f, pf = fidx // 128, (fidx // 64) % 2, fidx % 64
    grpW = 2 * (4 * kf[None, :] + jq[:, None]) + g2f[None, :]
    pW = np.broadcast_to(pf[None, :], (128, 512))
    cW = np.broadcast_to(cq_[:, None], (128, 512))
    maskW = (g2q[:, None] == g2f[None, :])
    ssm_W = np.zeros((DEPTH, 5, 128, 512), np.float32)
    ssm_W[:, 0] = are[:, grpW, pW]
    ssm_W[:, 1] = aim[:, grpW, pW]
    ssm_W[:, 2] = ldt[:, grpW]
    ssm_W[:, 3] = np.where(maskW[None], Bre[:, grpW, pW, cW], 0.0)
    ssm_W[:, 4] = np.where(maskW[None], Bim[:, grpW, pW, cW], 0.0)
    g2v, pv = q // 64, q % 64
    pairf, g2f2, cf2 = fidx // 32, (fidx // 16) % 2, fidx % 16
    grpV = 2 * pairf[None, :] + g2v[:, None]
    pV = np.broadcast_to(pv[:, None], (128, 512))
    cV = np.broadcast_to(cf2[None, :], (128, 512))
    maskV = (g2v[:, None] == g2f2[None, :])
    ssm_V = np.zeros((DEPTH, 7, 128, 512), np.float32)
    ssm_V[:, 0] = are[:, grpV, pV]
    ssm_V[:, 1] = aim[:, grpV, pV]
    ssm_V[:, 2] = ldt[:, grpV]
    ssm_V[:, 3] = np.where(maskV[None], Bre[:, grpV, pV, cV], 0.0)
    ssm_V[:, 4] = np.where(maskV[None], Bim[:, grpV, pV, cV], 0.0)
    ssm_V[:, 5] = np.where(maskV[None], Cre[:, grpV, cV, pV], 0.0)
    ssm_V[:, 6] = np.where(maskV[None], Cim[:, grpV, cV, pV], 0.0)
    ddiag = np.zeros((DEPTH, 128, 4, 128), np.float32)
    dd = inp["ssm_d"].reshape(DEPTH, 4, 128)
    for k in range(4):
        ddiag[:, q, k, q] = dd[:, k, :]
    ident, bias, cf = _const_tables()
    return dict(w_in_r=w_in_r, w_br_r=w_br_r, w_out_r=w_out_r, w_glu_r=w_glu_r, w_pool_r=w_pool_r, w_ada_r=w_ada_r,
                colvec=colvec, bgate_rep=bgate_rep, finalg_rep=finalg_rep, ssm_W=ssm_W, ssm_V=ssm_V, ddiag=ddiag,
                ident=ident, biasT=bias, pool_cf=cf)


_NC_CACHE = {}


def _get_nc(key, **kw):
    if key not in _NC_CACHE:
        _NC_CACHE[key] = build_program(**kw)
    return _NC_CACHE[key]


def kernel(**inputs):
    inp = {k: np.asarray(v) for k, v in inputs.items()}
    shared = _shared_layout(inp)
    x = np.ascontiguousarray(inp["x"], dtype=np.float32)
    c = np.asarray(inp["c"], dtype=np.float32)
    in_maps = []
    for b in range(NCORES):
        m = dict(shared)
        m["x"] = x[b]
        m["cT"] = np.ascontiguousarray(c[b].reshape(8, 128).T)
        in_maps.append(m)
    nc = _get_nc("full")
    res = run_bass_kernel_spmd(nc, in_maps, core_ids=list(range(NCORES)))
    return np.stack([r["out"] for r in res.results], axis=0).astype(np.float32)
```

```python
import math
import os
import types
from contextlib import ExitStack
import numpy as np
import concourse.bass as bass
import concourse.mybir as mybir
from concourse.bass_utils import run_bass_kernel_spmd

F32 = mybir.dt.float32
BF16 = mybir.dt.bfloat16
I32 = mybir.dt.int32
AF = mybir.ActivationFunctionType
ALU = mybir.AluOpType

D = 1024
SEQ = 4096
TT = 512
NT = SEQ // TT
NB = TT // 8
DEPTH = 2
NCORES = 8
EPS = 1e-6
POOL_W = (2, 4, 8, 16)
WIN_COLS = 512 + 384 + 512 + 1024 + 1024 + 8 * 384
ENGS = ("pe", "dve", "act", "pool", "sp")


class Prog:
    def __init__(self, nc, stack, same_engine_sync=True):
        self.nc = nc
        self.stack = stack
        self.q = {e: [] for e in ENGS}
        self.esem = {e: stack.enter_context(nc.semaphore("prog_" + e)) for e in ENGS if e != "sp"}
        self.cnt = {e: 0 for e in ENGS}
        self.waited = {e: {} for e in ENGS}
        self.last_w = {}
        self.readers = {}
        self.dsem = {}
        self.dval = {}
        self.same_engine_sync = same_engine_sync

    def _need(self, eng, tokens):
        out = {}
        for item in tokens:
            if item is None:
                continue
            if len(item) == 2:
                tok, kind = item
                if tok is None:
                    continue
            else:
                tok, kind = item, "raw"
            key, sem, val, src = tok
            if src == eng:
                if eng == "pe" or not self.same_engine_sync:
                    continue
                if kind != "raw" and os.environ.get("SES_RAW_ONLY", "1") == "1":
                    continue
            if self.waited[eng].get(key, 0) >= val:
                continue
            if key not in out or out[key][1] < val:
                out[key] = (sem, val)
        for key, (sem, val) in out.items():
            self.waited[eng][key] = val
            self.q[eng].append(lambda e, sem=sem, val=val: e.wait_ge(sem, val))

    def _deps(self, reads, writes):
        toks = []
        for b in reads:
            toks.append((self.last_w.get(b), "raw"))
        for b in writes:
            toks.append((self.last_w.get(b), "waw"))
            toks.extend((r_, "war") for r_ in self.readers.get(b, {}).values())
        return toks

    def _commit(self, tok, reads, writes):
        for b in reads:
            self.readers.setdefault(b, {})[tok[0]] = tok
        for b in writes:
            self.last_w[b] = tok
            self.readers[b] = {}

    @staticmethod
    def _snap(fn):
        if fn.__closure__ is None:
            return fn
        cells = []
        for c in fn.__closure__:
            try:
                cells.append(types.CellType(c.cell_contents))
            except ValueError:
                cells.append(c)
        return types.FunctionType(fn.__code__, fn.__globals__, fn.__name__, fn.__defaults__, tuple(cells))

    def op(self, eng, fn, reads=(), writes=()):
        fn = self._snap(fn)
        self._need(eng, self._deps(reads, writes))
        self.cnt[eng] += 1
        val = self.cnt[eng]
        sem = self.esem[eng]
        self.q[eng].append(lambda e, fn=fn, sem=sem: fn(e).then_inc(sem, 1))
        self._commit(("E" + eng, sem, val, eng), reads, writes)

    def dma(self, queue, out, in_, reads=(), writes=(), group=None):
        self._need(queue, self._deps(reads, writes))
        g = group if group is not None else (writes[0] if writes else reads[0])
        if g not in self.dsem:
            self.dsem[g] = self.stack.enter_context(self.nc.semaphore("dma_%d" % len(self.dsem)))
            self.dval[g] = 0
        self.dval[g] += 16
        sem, val = self.dsem[g], self.dval[g]
        self.q[queue].append(lambda e, out=out, in_=in_, sem=sem: e.dma_start(out=out, in_=in_).then_inc(sem, 16))
        self._commit(("D" + str(g), sem, val, "dma"), reads, writes)

    def barrier(self, exclude=()):
        toks = [("E" + x, self.esem[x], self.cnt[x], "bar") for x in self.esem if self.cnt[x] > 0]
        toks += [("D" + str(g), self.dsem[g], self.dval[g], "dma") for g in self.dsem if g not in exclude]
        for e in ENGS:
            self._need(e, toks)

    def wait_all(self, eng, bufs):
        self._need(eng, [self.last_w.get(b) for b in bufs])

    def emit(self):
        with self.nc.Block() as block:
            @block.tensor
            def _(e):
                for f in self.q["pe"]:
                    f(e)

            @block.vector
            def _(e):
                for f in self.q["dve"]:
                    f(e)

            @block.scalar
            def _(e):
                for f in self.q["act"]:
                    f(e)

            @block.gpsimd
            def _(e):
                for f in self.q["pool"]:
                    f(e)

            @block.sync
            def _(e):
                for f in self.q["sp"]:
                    f(e)


def build_program(layers=(0, 1), final_norm=True, branches=(0, 1, 2), ntiles=NT, stage=9, debug=False):
    nc = bass.Bass("TRN2", target_bir_lowering=False)
    NL = len(layers)
    taps = {}

    def tap(name, ap, reads, cond=True):
        if not (debug and cond) or name in taps:
            return
        dt_ = ap.dtype
        d = nc.dram_tensor("dbg_" + name, list(ap.shape), dt_, kind="ExternalOutput").ap()
        taps[name] = d
        p.dma("sp", d, ap, reads=list(reads), writes=["dbg_" + name])

    def din(name, shape):
        return nc.dram_tensor(name, list(shape), F32, kind="ExternalInput").ap()

    x_d = din("x", [SEQ, D])
    out_d = nc.dram_tensor("out", [SEQ, D], F32, kind="ExternalOutput").ap()
    win_d = din("w_in_r", [DEPTH, 128, 8, WIN_COLS])
    wbr_d = din("w_br_r", [DEPTH, 8, 128, 1536])
    wout_d = din("w_out_r", [DEPTH, 128, 8, 1024])
    wglu_d = din("w_glu_r", [DEPTH, 128, 4, 512])
    wpool_d = din("w_pool_r", [DEPTH, 128, 4, 128])
    wada_d = din("w_ada_r", [DEPTH, 128, 8, 3072])
    colv_d = din("colvec", [DEPTH, 128, 44])
    bgate_d = din("bgate_rep", [DEPTH, 128, 1024])
    fing_d = din("finalg_rep", [128, 1024])
    cT_d = din("cT", [128, 8])
    ssmW_d = din("ssm_W", [DEPTH, 5, 128, 512])
    ssmV_d = din("ssm_V", [DEPTH, 7, 128, 512])
    ddiag_d = din("ddiag", [DEPTH, 128, 4, 128])
    ident_d = din("ident", [128, 128])
    bias_d = din("biasT", [128, 2, 8, 128])
    cf_d = din("pool_cf", [128, 4, 16])
    win_f, wbr_f, wout_f, wglu_f = win_d, wbr_d, wout_d, wglu_d
    win_d = nc.dram_tensor("win_b", [DEPTH, 128, 8, WIN_COLS], BF16, kind="Internal").ap()
    wbr_d = nc.dram_tensor("wbr_b", [DEPTH, 8, 128, 1536], BF16, kind="Internal").ap()
    wout_d = nc.dram_tensor("wout_b", [DEPTH, 128, 8, 1024], BF16, kind="Internal").ap()
    wglu_d = nc.dram_tensor("wglu_b", [DEPTH, 128, 4, 512], BF16, kind="Internal").ap()
    tab_d = nc.dram_tensor("ssm_tab", [DEPTH, 128, 21504], BF16, kind="Internal").ap()
    rot_d = nc.dram_tensor("ssm_rot", [DEPTH, 128, 2 * 16 * NB], F32, kind="Internal").ap()

    with ExitStack() as st:
        sb_bytes = [0]

        def sb(name, shape, dt=F32):
            n = 1
            for d_ in shape[1:]:
                n *= d_
            sb_bytes[0] += n * (2 if dt == BF16 else 4)
            return st.enter_context(nc.sbuf_tensor("s_" + name, list(shape), dt))

        p = Prog(nc, st, same_engine_sync=not os.environ.get("NOSES"))
        psum = st.enter_context(nc.psum_tensor("psum", [128, 7, 512], F32))
        psT = st.enter_context(nc.psum_tensor("psT", [128, 1024], BF16))
        bank_rr = [0]
        reserved = set()

        def bank():
            while True:
                b = bank_rr[0] % 7
                bank_rr[0] += 1
                if b not in reserved:
                    return b

        xs = sb("xs", [128, 4, D])
        hT = sb("hT", [128, 8, TT], BF16)
        NWB = 5
        wb = [sb("wb%d" % i, [128, 4096], BF16) for i in range(NWB)]
        wb_rr = [0]
        qz = sb("qz", [128, 8, TT], BF16)
        qT = qz[:, 0:4, :]
        zT = qz[:, 4:8, :]
        mrg = qz
        kkT = [sb("kkT%d" % l, [128, 2, 128 + TT], BF16) for l in range(NL)]
        vtok = [sb("vtok%d" % l, [128, 5, 128], BF16) for l in range(NL)]
        yT = [sb("yT%d" % i, [128, 4, TT], BF16) for i in range(3)]
        Eb = [sb("Eb%d" % i, [128, 512], BF16) for i in range(2)]
        gsb = [sb("gsb%d" % i, [128, 512]) for i in range(2)]
        uS = sb("uS", [128, 4, TT], BF16)
        uP = sb("uP", [128, 4, 16 + TT], BF16)
        uPh = [sb("uPh%d" % l, [128, 4, 16], BF16) for l in range(NL)]
        sA = sb("sA", [128, 16, 2, NB])
        sB = sb("sB", [128, 16, 2, NB])
        sT1 = sb("sT1", [128, 16, NB])
        sT2 = sb("sT2", [128, 16, NB])
        Xbf = sb("Xbf", [128, 16, 2, NB + 1], BF16)
        carry = [sb("carry%d" % l, [128, 16, 2, 1]) for l in range(NL)]
        rot = sb("rot", [128, 2, 16, NB])
        rcos = [rot[:, 0] for l in range(NL)]
        rsin = [rot[:, 1] for l in range(NL)]
        r8 = [sb("r8_%d" % l, [128, 16]) for l in range(NL)]
        tabs = sb("tabs", [128, 13312], BF16)
        BLv = tabs[:, 0:8192].rearrange("p (k i r m) -> p k i r m", k=4, i=8, r=2)
        CLv = tabs[:, 0:9216].rearrange("p (m r f) -> p m r f", m=9, r=2)
        KLv = tabs[:, 9216:13312].rearrange("p (k d m) -> p k d m", k=4, d=8)
        gate_rep = [sb("gate_rep%d" % l, [128, D]) for l in range(NL)]
        fing = sb("fing", [128, D])
        xo = sb("xo", [128, D])
        biasT = sb("biasT", [128, 2, 8, 128], BF16)
        ident = sb("ident", [128, 128], BF16)
        ones = sb("ones", [128, 64], BF16)
        cf = sb("cf", [128, 4, 16])
        Wlag = [sb("Wlag%d" % l, [128, 4, 128], BF16) for l in range(NL)]
        W0 = [sb("W0_%d" % l, [128, 4, 128], BF16) for l in range(NL)]
        colv = [sb("colv%d" % l, [128, 44]) for l in range(NL)]
        aT = [sb("aT%d" % l, [128, 8]) for l in range(NL)]
        shT = [sb("shT%d" % l, [128, 8]) for l in range(NL)]
        sinkexp = [sb("sinkexp%d" % l, [128, 4]) for l in range(NL)]
        ssq = sb("ssq", [128, 8])
        rstd = sb("rstd", [128, 8])
        xhat = sb("xhat", [128, D], BF16)
        junk = gsb[0][:].bitcast(BF16)
        dens2 = [sb("dens%d" % i, [128, 128]) for i in range(2)]
        ytmp2 = [sb("ytmp%d" % i, [128, 128]) for i in range(2)]
        ft1 = sb("ft1", [128, 512])
        ft2 = sb("ft2", [128, 512])
        ft3 = sb("ft3", [128, 512])
        pc1 = sb("pc1", [128, 16])
        epsc = sb("epsc", [128, 1])
        bbv_sb = sb("bbv", [128, 2, 512], BF16)

        p.dma("pool", ident[:], ident_d, writes=["ident"])
        p.dma("pool", biasT[:], bias_d, writes=["biasT"])
        for l in layers:
            for kc in range(8):
                p.dma("pool", win_d[l][:, kc, :], win_f[l][:, kc, :], writes=["cw%d" % l])
            for ft in range(8):
                p.dma("pool", wbr_d[l, ft], wbr_f[l, ft], writes=["cw%d" % l])
            p.dma("pool", wout_d[l], wout_f[l], writes=["cw%d" % l])
            p.dma("pool", wglu_d[l], wglu_f[l], writes=["cw%d" % l])
        p.dma("sp", cf[:], cf_d, writes=["cf"])
        p.op("dve", lambda e: e.memset(ones[:], 1.0), writes=["ones"])
        p.op("dve", lambda e: e.memset(epsc[:], EPS), writes=["epsc"])
        for li in range(NL):
            p.op("dve", lambda e, li=li: e.memset(carry[li][:], 0.0), writes=["carry%d" % li])
            p.op("dve", lambda e, li=li: e.memset(uPh[li][:], 0.0), writes=["uPh%d" % li])

        class _V:
            def __init__(self, ap):
                self.ap = ap
            def __getitem__(self, k):
                return self.ap[k]
        cTs = sb("cTs", [128, 8])
        cact = sb("cact", [128, 8])
        crep = _V(sA[:].rearrange("p a b c -> p (a b c)")[:, 0:1024].rearrange("p (k m) -> p k m", k=8))
        p.dma("sp", cTs[:], cT_d, writes=["cTs"])
        p.op("act", lambda e: e.activation(out=cact[:], in_=cTs[:], func=AF.Silu), reads=["cTs"], writes=["cact"])
        p.op("dve", lambda e: e.tensor_copy(out=crep[:], in_=cact[:].unsqueeze(2).to_broadcast([128, 8, 128])),
             reads=["cact"], writes=["crep"])
        xs_flat = xs[:].rearrange("p s d -> p (s d)")
        stg = [xs_flat[:, 0:2048].rearrange("p (k c) -> p k c", k=8), xs_flat[:, 2048:4096].rearrange("p (k c) -> p k c", k=8),
               wb[3][:].bitcast(F32).rearrange("p (k c) -> p k c", k=8), wb[4][:].bitcast(F32).rearrange("p (k c) -> p k c", k=8)]
        stg_rr = [0]
        wst = _V(stg[0])
        modT2 = [sb("modT%d" % li_, [128, 16]) for li_ in range(NL)]

        def ada_mm(li, l):
            p.dma("sp", colv[li][:], colv_d[l], writes=["colv%d" % li])
            p.dma("sp", gate_rep[li][:], bgate_d[l], writes=["gate_rep%d" % li])
            for n2 in range(12):
                si = stg_rr[0] % 4
                stg_rr[0] += 1
                sg, sk_ = stg[si], "wst%d" % si
                p.dma("sp", sg, wada_d[l][:, :, n2 * 256:(n2 + 1) * 256], writes=[sk_])
                if n2 < 8:
                    for j in range(2):
                        col = li * 16 + n2 * 2 + j
                        for kc in range(8):
                            p.op("pe", lambda e, j=j, kc=kc, col=col: e.matmul(
                                psum[:, 0, col:col + 1], lhsT=sg[:, kc, j * 128:(j + 1) * 128], rhs=cact[:, kc:kc + 1],
                                start=(kc == 0), stop=(kc == 7)), reads=[sk_, "cact"], writes=["ps0"])
                else:
                    bk_ = 1 + 2 * li + (n2 - 8) // 2
                    c0_ = ((n2 - 8) % 2) * 256
                    for kc in range(8):
                        p.op("pe", lambda e, kc=kc, bk_=bk_, c0_=c0_: e.matmul(
                            psum[:, bk_, c0_:c0_ + 256], lhsT=crep[:, kc, :], rhs=sg[:, kc, :], start=(kc == 0), stop=(kc == 7)),
                            reads=[sk_, "crep"], writes=["ps%d" % bk_])

        def ada_evac(li, l):
            modT = modT2[li]
            p.op("act", lambda e: e.activation(out=modT[:], in_=psum[:, 0, li * 16:(li + 1) * 16], func=AF.Copy),
                 reads=["ps0"], writes=["modT%d" % li])
            for h_ in range(2):
                p.op("dve", lambda e, h_=h_: e.tensor_tensor(
                    out=gate_rep[li][:, h_ * 512:(h_ + 1) * 512], in0=psum[:, 1 + 2 * li + h_, :],
                    in1=gate_rep[li][:, h_ * 512:(h_ + 1) * 512], op=ALU.add),
                    reads=["ps%d" % (1 + 2 * li + h_), "gate_rep%d" % li], writes=["gate_rep%d" % li])
            p.op("dve", lambda e, li=li: e.tensor_tensor(out=shT[li][:], in0=modT[:, 0:8], in1=colv[li][:, 8:16], op=ALU.add),
                 reads=["modT%d" % li, "colv%d" % li], writes=["shT%d" % li])
            p.op("dve", lambda e, li=li: e.tensor_tensor(out=aT[li][:], in0=modT[:, 8:16], in1=colv[li][:, 16:24], op=ALU.add),
                 reads=["modT%d" % li, "colv%d" % li], writes=["aT%d" % li])
            p.op("dve", lambda e, li=li: e.scalar_tensor_tensor(out=aT[li][:], in0=aT[li][:], scalar=1.0, in1=colv[li][:, 0:8],
                                                               op0=ALU.add, op1=ALU.mult),
                 reads=["aT%d" % li, "colv%d" % li], writes=["aT%d" % li])
            p.op("act", lambda e, li=li: e.activation(out=sinkexp[li][:], in_=colv[li][:, 40:44], func=AF.Exp),
                 reads=["colv%d" % li], writes=["sinkexp%d" % li])
            wps = fing[:, 0:512].rearrange("p (g o) -> p g o", g=4)
            p.dma("sp", wps, wpool_d[l], writes=["fingstage"])
            for gi, w in enumerate(POOL_W):
                p.op("act", lambda e, li=li, gi=gi, w=w: e.activation(out=Wlag[li][:, gi, :], in_=wps[:, gi, :], func=AF.Copy,
                                                                   scale=1.0 / w), reads=["fingstage"], writes=["Wlag%d" % li])
                p.op("act", lambda e, li=li, gi=gi, w=w: e.activation(out=W0[li][:, gi, :], in_=wps[:, gi, :], func=AF.Copy,
                                                                   scale=1.0 / w - 1.0), reads=["fingstage"], writes=["W0_%d" % li])

        if 1 in branches:
            NTMP = 20
            tm = []
            for i in range(12):
                tm.append(wb[i // 4][:].bitcast(F32)[:, (i % 4) * 512:(i % 4 + 1) * 512])
            for i in range(6):
                tm.append(yT[i // 2][:].rearrange("p a b -> p (a b)").bitcast(F32)[:, (i % 2) * 512:(i % 2 + 1) * 512])
            for i in range(2):
                tm.append(uS[:].rearrange("p a b -> p (a b)").bitcast(F32)[:, i * 512:(i + 1) * 512])
            tm = [_V(a) for a in tm]
            tki = _V(sB[:].rearrange("p a b c -> p (a b c)").bitcast(I32)[:, 0:512])
            TWO_PI = 2.0 * math.pi
            C1 = 6.28125
            C2 = TWO_PI - C1

            def tt(eng, o, a, b_, op, okey, akey, bkey):
                p.op(eng, lambda e: e.tensor_tensor(out=o, in0=a, in1=b_, op=op), reads=[akey, bkey], writes=[okey])

            def ts(eng, o, a, s1, s2, op0, op1, okey, akey):
                if s2 is None:
                    p.op(eng, lambda e: e.tensor_scalar(out=o, in0=a, scalar1=s1, scalar2=None, op0=op0), reads=[akey], writes=[okey])
                else:
                    p.op(eng, lambda e: e.tensor_scalar(out=o, in0=a, scalar1=s1, scalar2=s2, op0=op0, op1=op1), reads=[akey], writes=[okey])

            def T(i):
                return tm[i][:], "tm%d" % i

            def sincos(ang_i, sin_i, cos_i, t_a, t_b):
                a, ak = T(ang_i)
                ta, tak = T(t_a)
                tb, tbk = T(t_b)
                s_, sk = T(sin_i)
                c_, ck = T(cos_i)
                ts("dve", ta, a, 1.0 / TWO_PI, None, ALU.mult, None, tak, ak)
                p.op("dve", lambda e: e.tensor_copy(out=tki[:], in_=ta), reads=[tak], writes=["tki"])
                p.op("dve", lambda e: e.tensor_copy(out=ta, in_=tki[:]), reads=["tki"], writes=[tak])
                p.op("dve", lambda e: e.scalar_tensor_tensor(out=tb, in0=ta, scalar=-C1, in1=a, op0=ALU.mult, op1=ALU.add),
                     reads=[tak, ak], writes=[tbk])
                p.op("dve", lambda e: e.scalar_tensor_tensor(out=tb, in0=ta, scalar=-C2, in1=tb, op0=ALU.mult, op1=ALU.add),
                     reads=[tak, tbk], writes=[tbk])
                ts("dve", ta, tb, 0.5, math.pi / 2, ALU.mult, ALU.add, tak, tbk)
                p.op("act", lambda e: e.activation(out=s_, in_=tb, func=AF.Sin, scale=0.5), reads=[tbk], writes=[sk])
                p.op("act", lambda e: e.activation(out=c_, in_=ta, func=AF.Sin), reads=[tak], writes=[ck])
                tt("dve", ta, s_, c_, ALU.mult, tak, sk, ck)
                tt("dve", tb, s_, s_, ALU.mult, tbk, sk, sk)
                ts("dve", s_, ta, 2.0, None, ALU.mult, None, sk, tak)
                ts("dve", c_, tb, -2.0, 1.0, ALU.mult, ALU.add, ck, tbk)

            def cmul(or_i, oi_i, ar_i, ai_i, br_i, bi_i, t1_i, t2_i):
                o_r, ork = T(or_i); o_i, oik = T(oi_i)
                a_r, ark = T(ar_i); a_i, aik = T(ai_i)
                b_r, brk = T(br_i); b_i, bik = T(bi_i)
                t1, t1k = T(t1_i); t2, t2k = T(t2_i)
                tt("dve", t1, a_r, b_r, ALU.mult, t1k, ark, brk)
                tt("dve", t2, a_i, b_i, ALU.mult, t2k, aik, bik)
                tt("dve", t1, t1, t2, ALU.subtract, t1k, t1k, t2k)
                tt("dve", t2, a_r, b_i, ALU.mult, t2k, ark, bik)
                tt("dve", o_i, a_i, b_r, ALU.mult, oik, aik, brk)
                tt("dve", o_i, o_i, t2, ALU.add, oik, oik, t2k)
                p.op("dve", lambda e: e.tensor_copy(out=o_r, in_=t1), reads=[t1k], writes=[ork])

            def cmul2(or_i, oi_i, ar_i, ai_i, br_i, bi_i):
                o_r, ork = T(or_i); o_i, oik = T(oi_i)
                a_r, ark = T(ar_i); a_i, aik = T(ai_i)
                b_r, brk = T(br_i); b_i, bik = T(bi_i)
                t1, t1k = T(12); t2, t2k = T(13); t3, t3k = T(14); t4, t4k = T(15)
                tt("dve", t1, a_r, b_r, ALU.mult, t1k, ark, brk)
                tt("dve", t2, a_i, b_i, ALU.mult, t2k, aik, bik)
                tt("dve", t3, a_r, b_i, ALU.mult, t3k, ark, bik)
                tt("dve", t4, a_i, b_r, ALU.mult, t4k, aik, brk)
                tt("dve", o_r, t1, t2, ALU.subtract, ork, t1k, t2k)
                tt("dve", o_i, t3, t4, ALU.add, oik, t3k, t4k)

            def lam_base(src_d, l):
                for i in range(3):
                    p.dma("act", tm[i][:], src_d[l, i], writes=["tm%d" % i])
                p.op("act", lambda e: e.activation(out=tm[9][:], in_=tm[2][:], func=AF.Exp), reads=["tm2"], writes=["tm9"])
                tt("dve", tm[7][:], tm[0][:], tm[9][:], ALU.mult, "tm7", "tm0", "tm9")
                tt("dve", tm[8][:], tm[1][:], tm[9][:], ALU.mult, "tm8", "tm1", "tm9")
                sincos(8, 10, 11, 12, 13)
                p.op("act", lambda e: e.activation(out=tm[9][:], in_=tm[7][:], func=AF.Exp), reads=["tm7"], writes=["tm9"])
                tt("dve", tm[3][:], tm[9][:], tm[11][:], ALU.mult, "tm3", "tm9", "tm11")
                tt("dve", tm[4][:], tm[9][:], tm[10][:], ALU.mult, "tm4", "tm9", "tm10")
                tt("dve", tm[12][:], tm[0][:], tm[0][:], ALU.mult, "tm12", "tm0", "tm0")
                tt("dve", tm[13][:], tm[1][:], tm[1][:], ALU.mult, "tm13", "tm1", "tm1")
                tt("dve", tm[12][:], tm[12][:], tm[13][:], ALU.add, "tm12", "tm12", "tm13")
                p.op("dve", lambda e: e.reciprocal(out=tm[12][:], in_=tm[12][:]), reads=["tm12"], writes=["tm12"])
                ts("dve", tm[13][:], tm[3][:], -1.0, None, ALU.add, None, "tm13", "tm3")
                tt("dve", tm[5][:], tm[13][:], tm[0][:], ALU.mult, "tm5", "tm13", "tm0")
                tt("dve", tm[14][:], tm[4][:], tm[1][:], ALU.mult, "tm14", "tm4", "tm1")
                tt("dve", tm[5][:], tm[5][:], tm[14][:], ALU.add, "tm5", "tm5", "tm14")
                tt("dve", tm[6][:], tm[4][:], tm[0][:], ALU.mult, "tm6", "tm4", "tm0")
                tt("dve", tm[14][:], tm[13][:], tm[1][:], ALU.mult, "tm14", "tm13", "tm1")
                tt("dve", tm[6][:], tm[6][:], tm[14][:], ALU.subtract, "tm6", "tm6", "tm14")
                tt("dve", tm[5][:], tm[5][:], tm[12][:], ALU.mult, "tm5", "tm5", "tm12")
                tt("dve", tm[6][:], tm[6][:], tm[12][:], ALU.mult, "tm6", "tm6", "tm12")

            def s5_tables(li, l, part):
                if part == 0:
                    lam_base(ssmW_d, l)
                    p.dma("act", tm[9][:], ssmW_d[l, 3], writes=["tm9"])
                    p.dma("act", tm[10][:], ssmW_d[l, 4], writes=["tm10"])
                    cmul(9, 10, 9, 10, 5, 6, 12, 13)
                    cur, nxt = (9, 10), (18, 19)
                    for m in range(8):
                        i = 7 - m
                        p.op("act", lambda e, i=i, cur=cur: e.activation(out=BLv[:, :, i, 0, :], in_=tm[cur[0]][:].rearrange("p (k m) -> p k m", k=4),
                                                                func=AF.Copy), reads=["tm%d" % cur[0]], writes=["tabs"])
                        p.op("act", lambda e, i=i, cur=cur: e.activation(out=BLv[:, :, i, 1, :], in_=tm[cur[1]][:].rearrange("p (k m) -> p k m", k=4),
                                                                func=AF.Copy), reads=["tm%d" % cur[1]], writes=["tabs"])
                        if m < 7:
                            cmul2(nxt[0], nxt[1], cur[0], cur[1], 3, 4)
                            cur, nxt = nxt, cur
                    p.dma("sp", tab_d[l][:, 0:8192], tabs[:, 0:8192], reads=["tabs"], writes=["tab_d%d" % l])
                if part == 1:
                    lam_base(ssmV_d, l)
                    ts("dve", tm[15][:], tm[8][:], 8.0, None, ALU.mult, None, "tm15", "tm8")
                    sincos(15, 16, 17, 12, 13)
                    p.op("act", lambda e, li=li: e.activation(out=r8[li][:], in_=tm[7][:, 0:512:32], func=AF.Exp, scale=8.0),
                         reads=["tm7"], writes=["r8_%d" % li])
                    p.op("dve", lambda e, li=li: e.tensor_copy(out=rcos[li][:, :, 0], in_=tm[17][:, 0:512:32]), reads=["tm17"], writes=["rot"])
                    p.op("dve", lambda e, li=li: e.tensor_copy(out=rsin[li][:, :, 0], in_=tm[16][:, 0:512:32]), reads=["tm16"], writes=["rot"])
                    ur = sb("ur%d" % li, [128, 16]); ui = sb("ui%d" % li, [128, 16]); uq = sb("uq%d" % li, [128, 16]); uw = sb("uw%d" % li, [128, 16])
                    p.op("dve", lambda e, li=li: e.tensor_copy(out=ur[:], in_=tm[17][:, 0:512:32]), reads=["tm17"], writes=["ur%d" % li])
                    p.op("dve", lambda e, li=li: e.tensor_copy(out=ui[:], in_=tm[16][:, 0:512:32]), reads=["tm16"], writes=["ui%d" % li])
                    n = 1
                    while n < NB:
                        urb = ur[:].unsqueeze(2).to_broadcast([128, 16, n])
                        uib = ui[:].unsqueeze(2).to_broadcast([128, 16, n])
                        c0 = rcos[li][:, :, 0:n]; s0 = rsin[li][:, :, 0:n]
                        c1 = rcos[li][:, :, n:2 * n]; s1 = rsin[li][:, :, n:2 * n]
                        t1 = sT1[:, :, 0:n]; t2 = sT2[:, :, 0:n]
                        ck, sk = "rot", "rot"
                        uk = ["ur%d" % li, "ui%d" % li]
                        p.op("dve", lambda e, c0=c0, urb=urb, t1=t1: e.tensor_tensor(out=t1, in0=c0, in1=urb, op=ALU.mult), reads=[ck] + uk, writes=["sT1"])
                        p.op("dve", lambda e, s0=s0, uib=uib, t2=t2: e.tensor_tensor(out=t2, in0=s0, in1=uib, op=ALU.mult), reads=[sk] + uk, writes=["sT2"])
                        p.op("dve", lambda e, c1=c1, t1=t1, t2=t2: e.tensor_tensor(out=c1, in0=t1, in1=t2, op=ALU.subtract), reads=["sT1", "sT2"], writes=[ck])
                        p.op("dve", lambda e, c0=c0, uib=uib, t1=t1: e.tensor_tensor(out=t1, in0=c0, in1=uib, op=ALU.mult), reads=[ck] + uk, writes=["sT1"])
                        p.op("dve", lambda e, s0=s0, urb=urb, t2=t2: e.tensor_tensor(out=t2, in0=s0, in1=urb, op=ALU.mult), reads=[sk] + uk, writes=["sT2"])
                        p.op("dve", lambda e, s1=s1, t1=t1, t2=t2: e.tensor_tensor(out=s1, in0=t1, in1=t2, op=ALU.add), reads=["sT1", "sT2"], writes=[sk])
                        p.op("dve", lambda e: e.tensor_tensor(out=uq[:], in0=ur[:], in1=ur[:], op=ALU.mult), reads=uk, writes=["uq%d" % li])
                        p.op("dve", lambda e: e.tensor_tensor(out=uw[:], in0=ui[:], in1=ui[:], op=ALU.mult), reads=uk, writes=["uw%d" % li])
                        p.op("dve", lambda e: e.tensor_tensor(out=uq[:], in0=uq[:], in1=uw[:], op=ALU.subtract), reads=["uq%d" % li, "uw%d" % li], writes=["uq%d" % li])
                        p.op("dve", lambda e: e.scalar_tensor_tensor(out=ui[:], in0=ur[:], scalar=2.0, in1=ui[:], op0=ALU.mult, op1=ALU.mult),
                             reads=uk, writes=["ui%d" % li])
                        p.op("dve", lambda e: e.tensor_copy(out=ur[:], in_=uq[:]), reads=["uq%d" % li], writes=["ur%d" % li])
                        n *= 2
                    p.dma("act", tm[9][:], ssmV_d[l, 3], writes=["tm9"])
                    p.dma("act", tm[10][:], ssmV_d[l, 4], writes=["tm10"])
                    cmul(9, 10, 9, 10, 5, 6, 12, 13)
                    bbv = bbv_sb
                    p.op("act", lambda e: e.activation(out=bbv[:, 0, :], in_=tm[9][:], func=AF.Copy), reads=["tm9"], writes=["bbv"])
                    p.op("act", lambda e: e.activation(out=bbv[:, 1, :], in_=tm[10][:], func=AF.Copy), reads=["tm10"], writes=["bbv"])
                    p.dma("act", tm[9][:], ssmV_d[l, 5], writes=["tm9"])
                    p.dma("act", tm[10][:], ssmV_d[l, 6], writes=["tm10"])
                    cur, nxt = (9, 10), (18, 19)
                    for m in range(9):
                        p.op("act", lambda e, m=m, cur=cur: e.activation(out=CLv[:, m, 0, :], in_=tm[cur[0]][:], func=AF.Copy), reads=["tm%d" % cur[0]], writes=["tabs"])
                        p.op("act", lambda e, m=m, cur=cur: e.activation(out=CLv[:, m, 1, :], in_=tm[cur[1]][:], func=AF.Copy, scale=-1.0), reads=["tm%d" % cur[1]], writes=["tabs"])
                        if m < 8:
                            cmul2(nxt[0], nxt[1], cur[0], cur[1], 3, 4)
                            cur, nxt = nxt, cur
                    p.op("dve", lambda e: e.memset(KLv, 0.0), writes=["tabs"])
                    psK = psum[:, 5:7, :].rearrange("p b (k d m) -> p (b k) d m", k=2, d=8)
                    dds = fing[:, 512:1024].rearrange("p (k m) -> p k m", k=4)
                    p.dma("act", dds, ddiag_d[l], writes=["fingstage2"])
                    for pair in range(16):
                        k, j = pair // 4, pair % 4
                        for dl in range(8):
                            for ri in range(2):
                                p.op("pe", lambda e, pair=pair, k=k, j=j, dl=dl, ri=ri: e.matmul(
                                    psK[32 * j:32 * j + 32, k, dl, :], lhsT=bbv[:, ri, pair * 32:(pair + 1) * 32],
                                    rhs=CLv[:, dl, ri, pair * 32:(pair + 1) * 32], start=(ri == 0), stop=(ri == 1),
                                    tile_position=(0, 32 * j)),
                                    reads=["bbv", "tabs"], writes=["ps5", "ps6"])
                    for j in range(4):
                        p.op("dve", lambda e, j=j: e.tensor_copy(out=KLv[32 * j:32 * j + 32, :, :, 32 * j:32 * j + 32],
                                                                 in_=psK[32 * j:32 * j + 32, :, :, :]), reads=["ps5", "ps6"], writes=["tabs"])
                    p.op("dve", lambda e: e.tensor_tensor(out=ft1[:].rearrange("p (k m) -> p k m", k=4), in0=KLv[:, :, 0, :], in1=dds, op=ALU.add),
                         reads=["tabs", "fingstage2"], writes=["ft1"])
                    p.op("dve", lambda e: e.tensor_copy(out=KLv[:, :, 0, :], in_=ft1[:].rearrange("p (k m) -> p k m", k=4)), reads=["ft1"], writes=["tabs"])
                    p.dma("sp", tab_d[l][:, 8192:21504], tabs[:], reads=["tabs"], writes=["tab_d%d" % l])
                    p.dma("sp", rot_d[l], rot[:].rearrange("p a b c -> p (a b c)"), reads=["rot"], writes=["rot_d%d" % l])

        for li, l in enumerate(layers):
            ada_mm(li, l)
        for li, l in enumerate(layers):
            if 1 in branches:
                s5_tables(li, l, 0)
                s5_tables(li, l, 1)
        for li, l in enumerate(layers):
            ada_evac(li, l)
        if os.environ.get("SBDBG"):
            print("SBUF bytes/partition:", sb_bytes[0])
        p.barrier(exclude=["cw%d" % l_ for l_ in range(DEPTH)])
        cur_l = [layers[0]]

        def load_w(src_ap, ncols_total, shape_view=None, ck=None):
            i = wb_rr[0] % NWB
            wb_rr[0] += 1
            key = "wb%d" % i
            dst = wb[i][:, 0:ncols_total]
            if shape_view is not None:
                dst = shape_view(dst)
            p.dma(os.environ.get("WQ", "pool"), dst, src_ap, reads=[ck or ("cw%d" % cur_l[0])], writes=[key])
            if os.environ.get("SYNCW"):
                p.barrier()
            return wb[i], key

        def proj_fm(w_ap, wkey, ncol0, evac):
            b = bank()
            for kc in range(8):
                p.op("pe", lambda e, b=b, kc=kc: e.matmul(psum[:, b, :], lhsT=w_ap[:, kc, ncol0:ncol0 + 128], rhs=hT[:, kc, :],
                                                          start=(kc == 0), stop=(kc == 7)), reads=[wkey] + HTK, writes=["ps%d" % b])
            evac(b)

        v3 = lambda a: a.rearrange("p (k c) -> p k c", k=8)
        XK = ["xs0", "xs1", "xs2", "xs3"]
        HTK = ["hT0", "hT1", "hT2", "hT3"]
        pend_u = {}

        def norm_sub(li, s):
            p.op("act", lambda e: e.activation(out=junk, in_=xs[:, s, :], func=AF.Square, accum_out=ssq[:, s:s + 1]),
                 reads=[XK[s]], writes=["gsb0", "ssq%d" % s])
            p.op("act", lambda e: e.activation(out=rstd[:, s:s + 1], in_=ssq[:, s:s + 1], func=AF.Ln, scale=1.0 / D, bias=epsc[:, 0:1]),
                 reads=["ssq%d" % s, "epsc"], writes=["rstd%d" % s])
            p.op("act", lambda e: e.activation(out=rstd[:, s:s + 1], in_=rstd[:, s:s + 1], func=AF.Exp, scale=-0.5), reads=["rstd%d" % s], writes=["rstd%d" % s])
            p.op("dve", lambda e: e.tensor_scalar(out=xhat[:], in0=xs[:, s, :], scalar1=rstd[:, s:s + 1], scalar2=None, op0=ALU.mult),
                 reads=[XK[s], "rstd%d" % s], writes=["xhat"])
            for kc in range(8):
                p.op("pe", lambda e, kc=kc: e.transpose(out=psT[:, kc * 128:(kc + 1) * 128], in_=xhat[:, kc * 128:(kc + 1) * 128],
                                                      identity=ident[:]), reads=["xhat", "ident"], writes=["psT"])
            for kc in range(8):
                if kc % 2 == 0:
                    p.op("act", lambda e, kc=kc: e.activation(out=hT[:, kc, s * 128:(s + 1) * 128], in_=psT[:, kc * 128:(kc + 1) * 128],
                                                            func=AF.Identity, scale=aT[li][:, kc:kc + 1], bias=shT[li][:, kc:kc + 1]),
                         reads=["psT", "aT%d" % li, "shT%d" % li], writes=["hT%d" % s])
                else:
                    p.op("dve", lambda e, kc=kc: e.tensor_scalar(out=hT[:, kc, s * 128:(s + 1) * 128], in0=psT[:, kc * 128:(kc + 1) * 128],
                                                               scalar1=aT[li][:, kc:kc + 1], scalar2=shT[li][:, kc:kc + 1],
                                                               op0=ALU.mult, op1=ALU.add),
                         reads=["psT", "aT%d" % li, "shT%d" % li], writes=["hT%d" % s])

        def final_sub(t0, s):
            p.op("act", lambda e: e.activation(out=junk, in_=xs[:, s, :], func=AF.Square, accum_out=ssq[:, 4 + s:5 + s]),
                 reads=[XK[s]], writes=["gsb0", "ssqf%d" % s])
            p.op("act", lambda e: e.activation(out=rstd[:, 4 + s:5 + s], in_=ssq[:, 4 + s:5 + s], func=AF.Ln, scale=1.0 / D, bias=epsc[:, 0:1]),
                 reads=["ssqf%d" % s, "epsc"], writes=["rstdf%d" % s])
            p.op("act", lambda e: e.activation(out=rstd[:, 4 + s:5 + s], in_=rstd[:, 4 + s:5 + s], func=AF.Exp, scale=-0.5), reads=["rstdf%d" % s], writes=["rstdf%d" % s])
            p.op("dve", lambda e: e.scalar_tensor_tensor(out=xo[:], in0=xs[:, s, :], scalar=rstd[:, 4 + s:5 + s], in1=fing[:],
                                                       op0=ALU.mult, op1=ALU.mult), reads=[XK[s], "rstdf%d" % s, "fing"], writes=["xo"])

        def store_sub(t0, s):
            if final_norm:
                p.dma("sp", out_d[t0 + s * 128:t0 + (s + 1) * 128, :], xo[:], reads=["xo"], writes=["out%d" % s])
            else:
                p.dma("sp", out_d[t0 + s * 128:t0 + (s + 1) * 128, :], xs[:, s, :], reads=[XK[s]], writes=["out%d" % s])

        def load_x(t, s):
            t0_ = t * TT
            p.dma("sp", xs[:, s, :], x_d[t0_ + s * 128:t0_ + (s + 1) * 128, :], writes=[XK[s]], group="xin%d" % s)

        def att_scores(li, t, it):
            s, hp = it // 4, it % 4
            has_a = not (t == 0 and s == 0)
            kv = hp // 2
            Ei = it % 2
            abl = [0, 1] if has_a else [1]
            lo = 0 if has_a else 128
            b0 = 2 * (it % 2)
            bh = [b0, b0 + 1]
            for hh in range(2):
                h = 2 * hp + hh
                p.op("pe", lambda e, hh=hh, h=h: e.matmul(
                    psum[:, bh[hh], lo:256], lhsT=ident[:],
                    rhs=(biasT[:, :, h, :] if lo == 0 else biasT[:, 1, h, :]), start=True, stop=False),
                    reads=["ident", "biasT"], writes=["ps%d" % bh[hh]])
            for ab in abl:
                kcol = s * 128 + ab * 128
                for hh in range(2):
                    p.op("pe", lambda e, hh=hh, ab=ab, kcol=kcol: e.matmul(
                        psum[:, bh[hh], ab * 128:(ab + 1) * 128],
                        lhsT=kkT[li][hh * 64:(hh + 1) * 64, kv, kcol:kcol + 128],
                        rhs=qT[hh * 64:(hh + 1) * 64, hp, s * 128:(s + 1) * 128], start=False, stop=(ab == 1)),
                        reads=["kkT%d" % li, "qT"], writes=["ps%d" % bh[hh]])
            p.op("act", lambda e: e.activation(
                out=Eb[Ei][:].rearrange("p (h c) -> p h c", h=2)[:, :, lo:256], in_=psum[:, b0:b0 + 2, lo:256], func=AF.Exp, scale=0.125),
                reads=["ps%d" % b0, "ps%d" % (b0 + 1)], writes=["Eb%d" % Ei])
            return dict(s=s, hp=hp, kv=kv, Ei=Ei, abl=abl, it=it)

        def att_pv(li, c):
            s, hp, kv, Ei, abl = c["s"], c["hp"], c["kv"], c["Ei"], c["abl"]
            b2 = 4 + c["it"] % 3
            for which in range(2):
                for ai, ab in enumerate(abl):
                    for hh in range(2):
                        p.op("pe", lambda e, hh=hh, which=which, ab=ab, ai=ai: e.matmul(
                            psum[hh * 64:(hh + 1) * 64, b2, which * 128:(which + 1) * 128],
                            lhsT=(vtok[li][:, s + ab, kv * 64:(kv + 1) * 64] if which == 0 else ones[:]),
                            rhs=Eb[Ei][:, hh * 256 + ab * 128: hh * 256 + ab * 128 + 128],
                            start=(ai == 0), stop=(ai == len(abl) - 1)),
                            reads=["vtok%d" % li, "ones", "Eb%d" % Ei], writes=["ps%d" % b2])
            dn, yt = dens2[Ei], ytmp2[Ei]
            p.op("act", lambda e: e.activation(out=dn[:], in_=psum[:, b2, 128:256], func=AF.Ln, bias=sinkexp[li][:, hp:hp + 1]),
                 reads=["ps%d" % b2, "sinkexp%d" % li], writes=["dens%d" % Ei])
            p.op("act", lambda e: e.activation(out=dn[:], in_=dn[:], func=AF.Exp, scale=-1.0), reads=["dens%d" % Ei], writes=["dens%d" % Ei])
            p.op("dve", lambda e: e.tensor_tensor(out=yt[:], in0=psum[:, b2, 0:128], in1=dn[:], op=ALU.mult),
                 reads=["ps%d" % b2, "dens%d" % Ei], writes=["ytmp%d" % Ei])
            p.op("dve", lambda e: e.tensor_tensor(out=yT[0][:, hp, s * 128:(s + 1) * 128], in0=yt[:],
                                                in1=zT[:, hp, s * 128:(s + 1) * 128], op=ALU.mult),
                 reads=["ytmp%d" % Ei, "zT"], writes=["yT0"])

        def attention(li, l, t):
            wq, kq = load_w(win_d[l][:, :, 0:512], 4096, v3)
            wq3 = v3(wq[:, 0:4096])
            for ft in range(4):
                proj_fm(wq3, kq, ft * 128, lambda b, ft=ft: p.op("act", lambda e: e.activation(
                    out=qT[:, ft, :], in_=psum[:, b, :], func=AF.Copy), reads=["ps%d" % b], writes=["qT"]))
            wk, kk_ = load_w(win_d[l][:, :, 512:896], 3072, v3)
            wk3 = v3(wk[:, 0:3072])
            for kv in range(2):
                proj_fm(wk3, kk_, kv * 128, lambda b, kv=kv: p.op("dve", lambda e: e.tensor_copy(
                    out=kkT[li][:, kv, 128:128 + TT], in_=psum[:, b, :]), reads=["ps%d" % b], writes=["kkT%d" % li]))
            for s in range(4):
                b = bank()
                for kc in range(8):
                    p.op("pe", lambda e, kc=kc: e.matmul(psum[:, b, 0:128], lhsT=hT[:, kc, s * 128:(s + 1) * 128],
                                                       rhs=wk3[:, kc, 256:384], start=(kc == 0), stop=(kc == 7)),
                         reads=[kk_, "hT%d" % s], writes=["ps%d" % b])
                p.op("dve", lambda e: e.tensor_copy(out=vtok[li][:, s + 1, :], in_=psum[:, b, 0:128]),
                     reads=["ps%d" % b], writes=["vtok%d" % li])
            wz, kz = load_w(win_d[l][:, :, 896:1408], 4096, v3)
            wz3 = v3(wz[:, 0:4096])
            for ft in range(4):
                proj_fm(wz3, kz, ft * 128, lambda b, ft=ft: p.op("act", lambda e: e.activation(
                    out=zT[:, ft, :], in_=psum[:, b, :], func=AF.Silu), reads=["ps%d" % b], writes=["zT"]))
            c0_ = (t == 0 and li == 0)
            tap("qT", qT, ["qT"], c0_)
            tap("kkT", kkT[li][:], ["kkT%d" % li], c0_)
            tap("vtok", vtok[li][:], ["vtok%d" % li], c0_)
            tap("zT_att", zT, ["zT"], c0_)
            prev = None
            for it in range(16):
                cur = att_scores(li, t, it)
                if prev is not None:
                    att_pv(li, prev)
                prev = cur
            att_pv(li, prev)
            tap("yT0", yT[0][:], ["yT0"], c0_)
            p.op("act", lambda e: e.activation(out=kkT[li][:, :, 0:128], in_=kkT[li][:, :, TT:TT + 128], func=AF.Copy),
                 reads=["kkT%d" % li], writes=["kkT%d" % li])
            p.op("act", lambda e: e.activation(out=vtok[li][:, 0, :], in_=vtok[li][:, 4, :], func=AF.Copy),
                 reads=["vtok%d" % li], writes=["vtok%d" % li])

        pskeys = ["ps3", "ps4", "ps5", "ps6"]
        psD5 = psum[:, 3:7, :].rearrange("p j (k r n) -> p j k r n", k=4, r=2)
        psDv = [psD5[:, :, :, ri_, :].rearrange("p j k n -> p k j n") for ri_ in range(2)]
        v4 = lambda a: a.rearrange("p (k j) n -> p k j n", k=4)

        def uproj_begin(l):
            wu, ku = load_w(win_d[l][:, :, 1408:1920], 4096, v3, ck="cw%d" % l)
            reserved.update({3, 4, 5, 6})
            return dict(w3=v3(wu[:, 0:4096]), key=ku)

        def uproj_sub(ctx, s):
            w3, ku = ctx["w3"], ctx["key"]
            for ft in range(4):
                for kc in range(8):
                    p.op("pe", lambda e, ft=ft, kc=kc: e.matmul(
                        psum[:, 3 + ft, s * 128:(s + 1) * 128], lhsT=w3[:, kc, ft * 128:(ft + 1) * 128], rhs=hT[:, kc, s * 128:(s + 1) * 128],
                        start=(kc == 0), stop=(kc == 7)), reads=[ku, "hT%d" % s], writes=["ps%d" % (3 + ft)])

        def uproj_end(ctx):
            for ft in range(4):
                p.op("act", lambda e, ft=ft: e.activation(out=uS[:, ft, :], in_=psum[:, 3 + ft, :], func=AF.Copy),
                     reads=["ps%d" % (3 + ft)], writes=["uS"])
            reserved.clear()

        def ssm_part1(li, l):
            ctx = pend_u.pop("ctx", None)
            if ctx is None:
                ctx = uproj_begin(l)
                for s in range(4):
                    uproj_sub(ctx, s)
            uproj_end(ctx)
            p.dma("sp", tabs[:, 0:8192], tab_d[l][:, 0:8192], reads=["tab_d%d" % l], writes=["tabs"])
            p.dma("sp", rot[:].rearrange("p a b c -> p (a b c)"), rot_d[l], reads=["rot_d%d" % l], writes=["rot"])
            for k in range(4):
                for ri in range(2):
                    for i in range(8):
                        for j in range(4):
                            p.op("pe", lambda e, k=k, j=j, ri=ri, i=i: e.matmul(
                                psD5[:, j, k, ri, :], lhsT=BLv[32 * j:32 * j + 32, k, i, ri, :], rhs=uS[32 * j:32 * j + 32, k, i:TT:8],
                                start=(i == 0), stop=(i == 7), tile_position=(32 * j, 0)),
                                reads=["tabs", "uS"], writes=[pskeys[j]])
            p.dma("sp", tabs[:], tab_d[l][:, 8192:21504], reads=["tab_d%d" % l], writes=["tabs"])
            rc, rs = v4(rcos[li]), v4(rsin[li])
            T1, T2 = v4(sT1[:]), v4(sT2[:])
            p.op("dve", lambda e: e.tensor_tensor(out=T1, in0=psDv[0], in1=rc, op=ALU.mult), reads=pskeys + ["rot"], writes=["sT1"])
            p.op("dve", lambda e: e.tensor_tensor(out=T2, in0=psDv[1], in1=rs, op=ALU.mult), reads=pskeys + ["rot"], writes=["sT2"])
            p.op("dve", lambda e: e.tensor_tensor(out=sA[:, :, 0, :], in0=sT1[:], in1=sT2[:], op=ALU.add), reads=["sT1", "sT2"], writes=["sA"])
            p.op("dve", lambda e: e.tensor_tensor(out=T1, in0=psDv[1], in1=rc, op=ALU.mult), reads=pskeys + ["rot"], writes=["sT1"])
            p.op("dve", lambda e: e.tensor_tensor(out=T2, in0=psDv[0], in1=rs, op=ALU.mult), reads=pskeys + ["rot"], writes=["sT2"])
            p.op("dve", lambda e: e.tensor_tensor(out=sA[:, :, 1, :], in0=sT1[:], in1=sT2[:], op=ALU.subtract), reads=["sT1", "sT2"], writes=["sA"])
            for pair in range(16):
                for ri in range(2):
                    p.op("dve", lambda e, pair=pair, ri=ri: e.tensor_tensor_scan(
                        out=sB[:, pair, ri, :], data0=r8[li][:, pair:pair + 1].to_broadcast([128, NB]), data1=sA[:, pair, ri, :],
                        initial=carry[li][:, pair, ri, :], op0=ALU.mult, op1=ALU.add),
                        reads=["sA", "r8_%d" % li, "carry%d" % li], writes=["sB"])
            p.op("dve", lambda e: e.tensor_tensor(out=sT1[:], in0=sB[:, :, 0, :], in1=rcos[li][:], op=ALU.mult), reads=["sB", "rot"], writes=["sT1"])
            p.op("dve", lambda e: e.tensor_tensor(out=sT2[:], in0=sB[:, :, 1, :], in1=rsin[li][:], op=ALU.mult), reads=["sB", "rot"], writes=["sT2"])
            p.op("dve", lambda e: e.tensor_tensor(out=sA[:, :, 0, :], in0=sT1[:], in1=sT2[:], op=ALU.subtract), reads=["sT1", "sT2"], writes=["sA"])
            p.op("dve", lambda e: e.tensor_tensor(out=sT1[:], in0=sB[:, :, 0, :], in1=rsin[li][:], op=ALU.mult), reads=["sB", "rot"], writes=["sT1"])
            p.op("dve", lambda e: e.tensor_tensor(out=sT2[:], in0=sB[:, :, 1, :], in1=rcos[li][:], op=ALU.mult), reads=["sB", "rot"], writes=["sT2"])
            p.op("dve", lambda e: e.tensor_tensor(out=sA[:, :, 1, :], in0=sT1[:], in1=sT2[:], op=ALU.add), reads=["sT1", "sT2"], writes=["sA"])
            p.op("act", lambda e: e.activation(out=Xbf[:, :, :, 0:1], in_=carry[li][:], func=AF.Copy), reads=["carry%d" % li], writes=["Xbf"])
            p.op("act", lambda e: e.activation(out=Xbf[:, :, :, 1:NB + 1], in_=sA[:], func=AF.Copy), reads=["sA"], writes=["Xbf"])
            p.op("dve", lambda e: e.tensor_copy(out=carry[li][:], in_=sA[:, :, :, NB - 1:NB]), reads=["sA", "Xbf"], writes=["carry%d" % li])

        def ssm_part2(li, l):
            wz, kz = load_w(win_d[l][:, :, 1920:2432], 4096, v3)
            wz3 = v3(wz[:, 0:4096])
            for ft in range(4):
                proj_fm(wz3, kz, ft * 128, lambda b, ft=ft: p.op("act", lambda e: e.activation(
                    out=zT[:, ft, :], in_=psum[:, b, :], func=AF.Silu), reads=["ps%d" % b], writes=["zT"]))
            wg, kg = load_w(wglu_d[l], 2048, lambda a: a.rearrange("p (k c) -> p k c", k=4))
            wg3 = wg[:, 0:2048].rearrange("p (k c) -> p k c", k=4)
            for k in range(4):
                bk = 3 + k
                for dl in range(8):
                    p.op("pe", lambda e, k=k, bk=bk, dl=dl: e.matmul(
                        psum[:, bk, :].rearrange("p (n i) -> p n i", i=8)[:, :, dl:8],
                        lhsT=KLv[:, k, dl, :],
                        rhs=uS[:, k, :].rearrange("p (n i) -> p n i", i=8)[:, :, 0:8 - dl],
                        start=(dl == 0), stop=False), reads=["tabs", "uS"], writes=["ps%d" % bk])
                for i in range(8):
                    for ri in range(2):
                        for j in range(4):
                            pair = 4 * k + j
                            last = (j == 3 and i == 7 and ri == 1)
                            p.op("pe", lambda e, k=k, bk=bk, j=j, pair=pair, i=i, ri=ri, last=last: e.matmul(
                                psum[32 * j:32 * j + 32, bk, i:TT:8], lhsT=CLv[:, i + 1, ri, pair * 32:(pair + 1) * 32],
                                rhs=Xbf[:, pair, ri, 0:NB], start=False, stop=last, tile_position=(0, 32 * j)),
                                reads=["tabs", "Xbf"], writes=["ps%d" % bk])
                p.op("act", lambda e, bk=bk: e.activation(out=ft1[:], in_=psum[:, bk, :], func=AF.Square), reads=["ps%d" % bk], writes=["ft1"])
                p.op("dve", lambda e: e.tensor_scalar(out=ft1[:], in0=ft1[:], scalar1=0.044715, scalar2=1.0, op0=ALU.mult, op1=ALU.add), reads=["ft1"], writes=["ft1"])
                p.op("dve", lambda e, bk=bk: e.tensor_tensor(out=ft1[:], in0=psum[:, bk, :], in1=ft1[:], op=ALU.mult), reads=["ps%d" % bk, "ft1"], writes=["ft1"])
                p.op("act", lambda e: e.activation(out=ft1[:], in_=ft1[:], func=AF.Sigmoid, scale=1.5957691216057308), reads=["ft1"], writes=["ft1"])
                p.op("dve", lambda e, bk=bk: e.tensor_tensor(out=ft2[:], in0=psum[:, bk, :], in1=ft1[:], op=ALU.mult), reads=["ps%d" % bk, "ft1"], writes=["ft2"])
                p.op("act", lambda e, k=k: e.activation(out=uP[:, k, 16:16 + TT], in_=ft2[:], func=AF.Copy), reads=["ft2"], writes=["uP"])
                p.op("dve", lambda e, k=k: e.tensor_tensor(out=qT[:, k, :], in0=ft2[:], in1=zT[:, k, :], op=ALU.mult), reads=["ft2", "zT"], writes=["qT"])
            return wg3, kg

        def ssm_part2b(li, l, wg3, kg):
            for ft in range(4):
                b = bank()
                for kc in range(4):
                    p.op("pe", lambda e, kc=kc, ft=ft: e.matmul(psum[:, b, :], lhsT=wg3[:, kc, ft * 128:(ft + 1) * 128], rhs=uP[:, kc, 16:16 + TT],
                                                              start=(kc == 0), stop=(kc == 3)), reads=[kg, "uP"], writes=["ps%d" % b])
                p.op("act", lambda e, ft=ft: e.activation(out=ft3[:], in_=psum[:, b, :], func=AF.Sigmoid, bias=colv[li][:, 36 + ft:37 + ft]),
                     reads=["ps%d" % b, "colv%d" % li], writes=["ft3"])
                p.op("dve", lambda e, ft=ft: e.tensor_tensor(out=yT[1][:, ft, :], in0=ft3[:], in1=qT[:, ft, :], op=ALU.mult), reads=["ft3", "qT"], writes=["yT1"])

        def pooling(li, l, t):
            wu, ku = load_w(win_d[l][:, :, 2432:2944], 4096, v3)
            wu3 = v3(wu[:, 0:4096])
            p.op("act", lambda e: e.activation(out=uP[:, :, 0:16], in_=uPh[li][:], func=AF.Copy), reads=["uPh%d" % li], writes=["uP"])
            for ft in range(4):
                proj_fm(wu3, ku, ft * 128, lambda b, ft=ft: p.op("act", lambda e: e.activation(
                    out=uP[:, ft, 16:16 + TT], in_=psum[:, b, :], func=AF.Copy), reads=["ps%d" % b], writes=["uP"]))
            p.op("act", lambda e: e.activation(out=uPh[li][:], in_=uP[:, :, TT:TT + 16], func=AF.Copy), reads=["uP"], writes=["uPh%d" % li])
            wz, kz = load_w(win_d[l][:, :, 2944:3456], 4096, v3)
            wz3 = v3(wz[:, 0:4096])
            for ft in range(4):
                proj_fm(wz3, kz, ft * 128, lambda b, ft=ft: p.op("act", lambda e: e.activation(
                    out=zT[:, ft, :], in_=psum[:, b, :], func=AF.Silu), reads=["ps%d" % b], writes=["zT"]))
            for gi, w in enumerate(POOL_W):
                b = bank()
                for kk in range(w):
                    p.op("pe", lambda e, gi=gi, kk=kk: e.matmul(
                        psum[:, b, :], lhsT=(W0[li][:, gi, :] if kk == 0 else Wlag[li][:, gi, :]), rhs=uP[:, gi, 16 - kk:16 - kk + TT],
                        start=(kk == 0), stop=(kk == w - 1)), reads=["W0_%d" % li, "Wlag%d" % li, "uP"], writes=["ps%d" % b])
                c0 = 0
                if t == 0:
                    c0 = 16
                    b2 = bank()
                    for kk in range(w):
                        p.op("pe", lambda e, gi=gi, kk=kk: e.matmul(
                            psum[:, b2, 0:16], lhsT=Wlag[li][:, gi, :], rhs=uP[:, gi, 16 - kk:32 - kk],
                            start=(kk == 0), stop=(kk == w - 1)), reads=["Wlag%d" % li, "uP"], writes=["ps%d" % b2])
                    p.op("dve", lambda e, gi=gi: e.tensor_tensor(out=pc1[:], in0=psum[:, b2, 0:16], in1=cf[:, gi, :], op=ALU.mult),
                         reads=["ps%d" % b2, "cf"], writes=["pc1"])
                    p.op("dve", lambda e: e.tensor_tensor(out=pc1[:], in0=psum[:, b, 0:16], in1=pc1[:], op=ALU.add),
                         reads=["ps%d" % b, "pc1"], writes=["pc1"])
                    p.op("dve", lambda e, gi=gi: e.scalar_tensor_tensor(out=yT[2][:, gi, 0:16], in0=pc1[:], scalar=colv[li][:, 32 + gi:33 + gi],
                                                                      in1=zT[:, gi, 0:16], op0=ALU.mult, op1=ALU.mult),
                         reads=["pc1", "colv%d" % li, "zT"], writes=["yT2"])
                p.op("dve", lambda e, gi=gi, c0=c0: e.scalar_tensor_tensor(
                    out=yT[2][:, gi, c0:TT], in0=psum[:, b, c0:TT], scalar=colv[li][:, 32 + gi:33 + gi], in1=zT[:, gi, c0:TT],
                    op0=ALU.mult, op1=ALU.mult), reads=["ps%d" % b, "colv%d" % li, "zT"], writes=["yT2"])

        def merge(li, l, t, hook=None):
            order = [b_ for b_ in (0, 2, 1) if b_ in branches]
            c0_ = (t == 0 and li == 0)
            for ft in range(8):
                wgt, kgt = load_w(win_d[l][:, :, 3456 + ft * 384: 3456 + (ft + 1) * 384], 3072, v3)
                wgt3 = v3(wgt[:, 0:3072])
                wbr, kbr = load_w(wbr_d[l, ft], 1536)
                first = True
                for bi_, br in enumerate(order):
                    if ft == 0 and br == 1 and hook is not None:
                        hook()
                    bg = bank()
                    for kc in range(8):
                        p.op("pe", lambda e, kc=kc, br=br: e.matmul(
                            psum[:, bg, :], lhsT=wgt3[:, kc, br * 128:(br + 1) * 128], rhs=hT[:, kc, :], start=(kc == 0), stop=(kc == 7)),
                            reads=[kgt] + HTK, writes=["ps%d" % bg])
                    gi_ = (ft * 3 + bi_) % 2
                    p.op("act", lambda e: e.activation(out=gsb[gi_][:], in_=psum[:, bg, :], func=AF.Sigmoid),
                         reads=["ps%d" % bg], writes=["gsb%d" % gi_])
                    bb = bank()
                    for kc in range(4):
                        p.op("pe", lambda e, kc=kc, br=br: e.matmul(
                            psum[:, bb, :], lhsT=wbr[:, (br * 4 + kc) * 128:(br * 4 + kc + 1) * 128],
                            rhs=yT[br][:, kc, :], start=(kc == 0), stop=(kc == 3)), reads=[kbr, "yT%d" % br], writes=["ps%d" % bb])
                    if first:
                        p.op("dve", lambda e: e.tensor_tensor(out=ft1[:], in0=psum[:, bb, :], in1=gsb[gi_][:], op=ALU.mult),
                             reads=["ps%d" % bb, "gsb%d" % gi_], writes=["ft1"])
                        first = False
                    else:
                        p.op("dve", lambda e: e.tensor_tensor(out=ft2[:], in0=psum[:, bb, :], in1=gsb[gi_][:], op=ALU.mult),
                             reads=["ps%d" % bb, "gsb%d" % gi_], writes=["ft2"])
                        p.op("dve", lambda e: e.tensor_tensor(out=ft1[:], in0=ft1[:], in1=ft2[:], op=ALU.add), reads=["ft1", "ft2"], writes=["ft1"])
                p.op("act", lambda e, ft=ft: e.activation(out=mrg[:, ft, :], in_=ft1[:], func=AF.Copy), reads=["ft1"], writes=["qT", "zT"])
            tap("yT1", yT[1][:], ["yT1"], c0_ and 1 in branches)
            tap("yT2", yT[2][:], ["yT2"], c0_ and 2 in branches)
            tap("mrg", mrg[:], ["qT", "zT"], c0_)

        def outproj(li, l, t, t0):
            ws = []
            for half in range(2):
                wo, ko = load_w(wout_d[l][:, half * 4:(half + 1) * 4, :], 4096, lambda a: a.rearrange("p (k c) -> p k c", k=4))
                wo3_ = wo[:, 0:4096].rearrange("p (k c) -> p k c", k=4)
                ws.append((wo3_, ko))
            uctx = None
            if 1 in branches and (li + 1 < NL or t + 1 < ntiles):
                uctx = uproj_begin(layers[li + 1] if li + 1 < NL else layers[0])
                pend_u["ctx"] = uctx
            for s in range(4):
                for dh in range(2):
                    b = bank()
                    for ft in range(8):
                        wsrc, wkey = ws[ft // 4]
                        p.op("pe", lambda e, ft=ft, dh=dh, wsrc=wsrc: e.matmul(
                            psum[:, b, :], lhsT=mrg[:, ft, s * 128:(s + 1) * 128], rhs=wsrc[:, ft % 4, dh * 512:(dh + 1) * 512],
                            start=(ft == 0), stop=(ft == 7)), reads=["qT", "zT", wkey], writes=["ps%d" % b])
                    p.op("dve", lambda e, dh=dh: e.tensor_tensor(out=ft3[:], in0=psum[:, b, :], in1=gate_rep[li][:, dh * 512:(dh + 1) * 512], op=ALU.mult),
                         reads=["ps%d" % b, "gate_rep%d" % li], writes=["ft3"])
                    p.op("dve", lambda e, dh=dh: e.tensor_tensor(out=xs[:, s, dh * 512:(dh + 1) * 512], in0=xs[:, s, dh * 512:(dh + 1) * 512], in1=ft3[:], op=ALU.add),
                         reads=[XK[s], "ft3"], writes=[XK[s]])
                if li + 1 < NL:
                    if s >= 1:
                        norm_sub(li + 1, s - 1)
                    if s >= 2 and uctx is not None:
                        uproj_sub(uctx, s - 2)
                else:
                    if s >= 1:
                        post_final(t, t0, s - 1)
                    if s >= 2 and t + 1 < ntiles:
                        norm_sub(0, s - 2)
                    if s >= 3 and t + 1 < ntiles and uctx is not None:
                        uproj_sub(uctx, s - 3)
            if li + 1 < NL:
                norm_sub(li + 1, 3)
                if uctx is not None:
                    uproj_sub(uctx, 2)
                    uproj_sub(uctx, 3)
            else:
                post_final(t, t0, 3)
                if t + 1 < ntiles:
                    norm_sub(0, 2)
                    if uctx is not None:
                        uproj_sub(uctx, 1)
                    norm_sub(0, 3)
                    if uctx is not None:
                        uproj_sub(uctx, 2)
                        uproj_sub(uctx, 3)

        def post_final(t, t0, s):
            if final_norm:
                final_sub(t0, s)
            store_sub(t0, s)
            if t + 1 < ntiles:
                load_x(t + 1, s)

        for t in range(ntiles):
            t0 = t * TT
            if t == 0:
                for s in range(4):
                    load_x(0, s)
                if final_norm:
                    p.dma("sp", fing[:], fing_d, writes=["fing"])
                for s in range(4):
                    norm_sub(0, s)
            for li, l in enumerate(layers):
                cur_l[0] = l
                c0_ = (t == 0 and li == 0)
                tap("hT", hT[:], HTK, c0_)
                tap("aT", aT[li][:], ["aT%d" % li], c0_)
                tap("shT", shT[li][:], ["shT%d" % li], c0_)
                tap("gate_rep", gate_rep[li][:], ["gate_rep%d" % li], c0_)
                tap("sinkexp", sinkexp[li][:], ["sinkexp%d" % li], c0_)
                if 1 in branches:
                    ssm_part1(li, l)
                if 2 in branches:
                    pooling(li, l, t)
                if 0 in branches:
                    attention(li, l, t)
                hook = None
                if 1 in branches:
                    wg3_, kg_ = ssm_part2(li, l)
                    hook = (lambda li=li, l=l, wg3_=wg3_, kg_=kg_: ssm_part2b(li, l, wg3_, kg_))
                merge(li, l, t, hook)
                outproj(li, l, t, t0)
        p.wait_all("sp", ["out%d" % s_ for s_ in range(4)] + ["dbg_" + n for n in taps])
        p.emit()
    return nc


def _const_tables():
    ident = np.eye(128, dtype=np.float32)
    BIG = -30000.0
    bias = np.zeros((128, 2, 8, 128), np.float32)
    j = np.arange(128)[:, None]
    i = np.arange(128)[None, :]
    ck, cq = j // 64, i // 64
    for h in range(8):
        slope = 2.0 ** (-(h + 1))
        da = (128 + i - j).astype(np.float32)
        a = -8.0 * slope * da
        a = np.where(ck >= cq, a, BIG)
        db = np.abs(i - j).astype(np.float32)
        b = -8.0 * slope * db
        b = np.where(ck <= cq, b, BIG)
        bias[:, 0, h, :] = a
        bias[:, 1, h, :] = b
    cf = np.zeros((128, 4, 16), np.float32)
    tt_ = np.arange(16)
    for gi, w in enumerate(POOL_W):
        cf[:, gi, :] = (w / np.minimum(tt_ + 1, w) - 1.0)[None, :]
    return ident, bias, cf


def _win_perm():
    cols = list(range(0, 512))
    cols += list(range(512, 576)) * 2 + list(range(576, 640)) * 2 + list(range(640, 768))
    cols += list(range(1792, 2304))
    cols += list(range(768, 1280)) + list(range(2304, 2816))
    cols += list(range(1280, 1792)) + list(range(2816, 3328))
    for ft in range(8):
        for br in range(3):
            cols += list(range(3328 + br * 1024 + ft * 128, 3328 + br * 1024 + (ft + 1) * 128))
    assert len(cols) == WIN_COLS
    return np.asarray(cols)


def _shared_layout(inp):
    f = lambda a: np.ascontiguousarray(a, dtype=np.float32)
    perm = _win_perm()
    w_in = inp["w_in"]
    w_in_r = f(w_in[:, :, perm].reshape(DEPTH, 8, 128, WIN_COLS).transpose(0, 2, 1, 3))
    wbr = np.stack([inp["w_br_att"], inp["w_br_ssm"], inp["w_br_pool"]], axis=1)
    w_br_r = f(wbr.reshape(DEPTH, 3, 4, 128, 8, 128).transpose(0, 4, 3, 1, 2, 5).reshape(DEPTH, 8, 128, 1536))
    w_out_r = f(inp["w_out"].reshape(DEPTH, 8, 128, 1024).transpose(0, 2, 1, 3))
    w_glu_r = f(inp["w_glu"].reshape(DEPTH, 4, 128, 512).transpose(0, 2, 1, 3))
    w_pool_r = f(inp["w_pool"].transpose(0, 2, 1, 3))
    w_ada_r = f(inp["w_ada"].reshape(DEPTH, 8, 128, 3072).transpose(0, 2, 1, 3))
    colvec = np.zeros((DEPTH, 128, 44), np.float32)
    colvec[:, :, 0:8] = inp["norm_g"].reshape(DEPTH, 8, 128).transpose(0, 2, 1)
    colvec[:, :, 8:32] = inp["b_ada"].reshape(DEPTH, 24, 128).transpose(0, 2, 1)
    colvec[:, :, 32:36] = inp["pool_scale"].reshape(DEPTH, 4, 128).transpose(0, 2, 1)
    colvec[:, :, 36:40] = inp["b_glu"].reshape(DEPTH, 4, 128).transpose(0, 2, 1)
    sk = inp["attn_sinks"].reshape(DEPTH, 4, 2)
    colvec[:, :, 40:44] = np.repeat(sk.transpose(0, 2, 1), 64, axis=1)
    bgate_rep = f(np.broadcast_to(inp["b_ada"][:, None, 2048:3072], (DEPTH, 128, 1024)))
    finalg_rep = f(np.broadcast_to(inp["final_g"][None, :], (128, 1024)))
    are, aim, ldt = inp["ssm_a_re"], inp["ssm_a_im"], inp["ssm_log_dt"]
    Bre, Bim, Cre, Cim = inp["ssm_b_re"], inp["ssm_b_im"], inp["ssm_c_re"], inp["ssm_c_im"]
    q = np.arange(128)
    fidx = np.arange(512)
    jq, g2q, cq_ = q // 32, (q // 16) % 2, q % 16
    kf, g2f, pf = fidx // 128, (fidx // 64) % 2, fidx % 64
    grpW = 2 * (4 * kf[None, :] + jq[:, None]) + g2f[None, :]
    pW = np.broadcast_to(pf[None, :], (128, 512))
    cW = np.broadcast_to(cq_[:, None], (128, 512))
    maskW = (g2q[:, None] == g2f[None, :])
    ssm_W = np.zeros((DEPTH, 5, 128, 512), np.float32)
    ssm_W[:, 0] = are[:, grpW, pW]
    ssm_W[:, 1] = aim[:, grpW, pW]
    ssm_W[:, 2] = ldt[:, grpW]
    ssm_W[:, 3] = np.where(maskW[None], Bre[:, grpW, pW, cW], 0.0)
    ssm_W[:, 4] = np.where(maskW[None], Bim[:, grpW, pW, cW], 0.0)
    g2v, pv = q // 64, q % 64
    pairf, g2f2, cf2 = fidx // 32, (fidx // 16) % 2, fidx % 16
    grpV = 2 * pairf[None, :] + g2v[:, None]
    pV = np.broadcast_to(pv[:, None], (128, 512))
    cV = np.broadcast_to(cf2[None, :], (128, 512))
    maskV = (g2v[:, None] == g2f2[None, :])
    ssm_V = np.zeros((DEPTH, 7, 128, 512), np.float32)
    ssm_V[:, 0] = are[:, grpV, pV]
    ssm_V[:, 1] = aim[:, grpV, pV]
    ssm_V[:, 2] = ldt[:, grpV]
    ssm_V[:, 3] = np.where(maskV[None], Bre[:, grpV, pV, cV], 0.0)
    ssm_V[:, 4] = np.where(maskV[None], Bim[:, grpV, pV, cV], 0.0)
    ssm_V[:, 5] = np.where(maskV[None], Cre[:, grpV, cV, pV], 0.0)
    ssm_V[:, 6] = np.where(maskV[None], Cim[:, grpV, cV, pV], 0.0)
    ddiag = np.zeros((DEPTH, 128, 4, 128), np.float32)
    dd = inp["ssm_d"].reshape(DEPTH, 4, 128)
    for k in range(4):
        ddiag[:, q, k, q] = dd[:, k, :]
    ident, bias, cf = _const_tables()
    return dict(w_in_r=w_in_r, w_br_r=w_br_r, w_out_r=w_out_r, w_glu_r=w_glu_r, w_pool_r=w_pool_r, w_ada_r=w_ada_r,
                colvec=colvec, bgate_rep=bgate_rep, finalg_rep=finalg_rep, ssm_W=ssm_W, ssm_V=ssm_V, ddiag=ddiag,
                ident=ident, biasT=bias, pool_cf=cf)


_NC_CACHE = {}


def _get_nc(key, **kw):
    if key not in _NC_CACHE:
        _NC_CACHE[key] = build_program(**kw)
    return _NC_CACHE[key]


def kernel(**inputs):
    inp = {k: np.asarray(v) for k, v in inputs.items()}
    shared = _shared_layout(inp)
    x = np.ascontiguousarray(inp["x"], dtype=np.float32)
    c = np.asarray(inp["c"], dtype=np.float32)
    in_maps = []
    for b in range(NCORES):
        m = dict(shared)
        m["x"] = x[b]
        m["cT"] = np.ascontiguousarray(c[b].reshape(8, 128).T)
        in_maps.append(m)
    nc = _get_nc("full")
    res = run_bass_kernel_spmd(nc, in_maps, core_ids=list(range(NCORES)))
    return np.stack([r["out"] for r in res.results], axis=0).astype(np.float32)
```
